# Optimizing a Trainium2 kernel written in Bass

```python
import math
import numpy as np
import jax
import jax.numpy as jnp
from jax import lax

D_MODEL = 1024
BATCH = 4
SEQ = 4096
DEPTH = 1
DEC_BATCH = 128
DEC_SEQ = 1
PAST_LEN = 2048
PAGE_SIZE = 128

SSM_EXPAND = 2
D_INNER = SSM_EXPAND * D_MODEL
SSM_HEAD_DIM = 64
N_SSM_HEADS = D_INNER // SSM_HEAD_DIM
SSM_GROUPS = 4
SSM_STATE = 128
SSM_CONV = 4
CONV_DIM = D_INNER + 2 * SSM_GROUPS * SSM_STATE
SSD_CHUNK = 128
N_ATT_HEADS = 16
ATT_HEAD_DIM = 64
N_KV_HEADS = 4
Q_PER_KV = N_ATT_HEADS // N_KV_HEADS
ATT_WIDTH = N_ATT_HEADS * ATT_HEAD_DIM
ATT_SCALE = ATT_HEAD_DIM ** -0.5
ROPE_DIM = ATT_HEAD_DIM // 4
ROPE_THETA = 500000.0
CMP_LEN = 32
CMP_STRIDE = 16
CMP_HIDDEN = 2 * ATT_HEAD_DIM
SEL_LEN = 64
SEL_TOPK = 16
WINDOW = 512
Q_BLOCK = 128
N_KV_STREAMS = 4
N_KV_PROJ = 6
D_FF = 2816
FFN_CONV = 3
EPS = 1e-6
NEG = -1e30
BIG = 1e30
TINY = 1e-30
IN_SIZES = (D_INNER, CONV_DIM, N_SSM_HEADS, ATT_WIDTH, N_KV_PROJ * N_KV_HEADS * ATT_HEAD_DIM, 3 * N_ATT_HEADS, 2 * D_MODEL)
D_IN_PROJ = sum(IN_SIZES)

kernel_name = "hybrid_ssd_nsa_convffn_adaln_step"


def _rmsnorm(x, w):
    x32 = x.astype(jnp.float32)
    y = x32 * lax.rsqrt(jnp.mean(x32 * x32, axis=-1, keepdims=True) + EPS)
    return y.astype(x.dtype) * w


def _rope(x, pos):
    half = ROPE_DIM // 2
    inv_freq = ROPE_THETA ** (-jnp.arange(half, dtype=jnp.float32) / half)
    ang = pos.astype(jnp.float32)[:, None] * inv_freq[None, :]
    shape = (pos.shape[0],) + (1,) * (x.ndim - 3) + (half,)
    cos = jnp.cos(ang).reshape(shape)
    sin = jnp.sin(ang).reshape(shape)
    x1 = x[..., :half].astype(jnp.float32)
    x2 = x[..., half:ROPE_DIM].astype(jnp.float32)
    rot = jnp.concatenate([x1 * cos - x2 * sin, x2 * cos + x1 * sin], axis=-1).astype(x.dtype)
    return jnp.concatenate([rot, x[..., ROPE_DIM:]], axis=-1)


def _causal_dwconv(xp, w, b):
    k = w.shape[0]
    t = xp.shape[1] - k + 1
    out = xp[:, 0:t] * w[0]
    for i in range(1, k):
        out = out + xp[:, i:i + t] * w[i]
    return out + b


def _masked_softmax(s, mask):
    s = jnp.where(mask, s.astype(jnp.float32), NEG)
    e = jnp.where(mask, jnp.exp(s - jnp.max(s, axis=-1, keepdims=True)), 0.0)
    return e / jnp.maximum(jnp.sum(e, axis=-1, keepdims=True), TINY)


def _ssd(x, dt, a, bm, cm, h0):
    b, t, nh, hp = x.shape
    g, n = bm.shape[2], bm.shape[3]
    r = nh // g
    l = SSD_CHUNK if t % SSD_CHUNK == 0 else t
    nc = t // l
    xdt = (x.astype(jnp.float32) * dt[..., None]).reshape(b, nc, l, g, r, hp)
    cs = jnp.cumsum((dt * a).reshape(b, nc, l, g, r), axis=2)
    bc = bm.astype(jnp.float32).reshape(b, nc, l, g, n)
    cc = cm.astype(jnp.float32).reshape(b, nc, l, g, n)
    causal = jnp.tril(jnp.ones((l, l), dtype=bool))[None, None, :, :, None, None]
    seg = cs[:, :, :, None] - cs[:, :, None, :]
    lmat = jnp.exp(jnp.where(causal, seg, -jnp.inf))
    cb = jnp.einsum("bclgn,bcsgn->bclsg", cc, bc)
    y_diag = jnp.einsum("bclsgr,bcsgrp->bclgrp", cb[..., None] * lmat, xdt)
    decay = jnp.exp(cs[:, :, -1:] - cs)
    states = jnp.einsum("bclgn,bclgrp->bcgrpn", bc, xdt * decay[..., None])
    chunk_decay = jnp.exp(cs[:, :, -1])

    def step(h, inp):
        st, dec = inp
        return h * dec[..., None, None] + st, h

    h_last, h_prev = lax.scan(step, h0.astype(jnp.float32).reshape(b, g, r, hp, n),
                              (jnp.moveaxis(states, 1, 0), jnp.moveaxis(chunk_decay, 1, 0)))
    h_prev = jnp.moveaxis(h_prev, 0, 1)
    y_off = jnp.einsum("bclgn,bcgrpn->bclgrp", cc, h_prev) * jnp.exp(cs)[..., None]
    y = (y_diag + y_off).reshape(b, t, nh, hp).astype(x.dtype)
    return y, h_last.reshape(b, nh, hp, n).astype(x.dtype)


def _mamba(z, xbc, dt_raw, conv_buf, h0, p):
    gn = SSM_GROUPS * SSM_STATE
    xp = jnp.concatenate([conv_buf.astype(xbc.dtype), xbc], axis=1)
    new_buf = xp[:, xp.shape[1] - (SSM_CONV - 1):]
    xc = jax.nn.silu(_causal_dwconv(xp, p["ssm_conv_w"], p["ssm_conv_b"]))
    b, t = xc.shape[0], xc.shape[1]
    xs = xc[..., :D_INNER].reshape(b, t, N_SSM_HEADS, SSM_HEAD_DIM)
    bm = xc[..., D_INNER:D_INNER + gn].reshape(b, t, SSM_GROUPS, SSM_STATE)
    cm = xc[..., D_INNER + gn:].reshape(b, t, SSM_GROUPS, SSM_STATE)
    dt = jax.nn.softplus((dt_raw + p["ssm_dt_bias"]).astype(jnp.float32))
    a = -jnp.exp(p["ssm_A_log"].astype(jnp.float32))
    y, h = _ssd(xs, dt, a, bm, cm, h0)
    y = (y + xs * p["ssm_D"][:, None]).reshape(b, t, D_INNER)
    y = _rmsnorm(y * jax.nn.silu(z), p["ssm_norm_w"])
    return y, new_buf, h


def _compress(k, pe, w1, w2):
    b, length = k.shape[0], k.shape[1]
    nc = (length - CMP_LEN) // CMP_STRIDE + 1
    nsub = CMP_LEN // CMP_STRIDE
    nj = nc + nsub - 1
    sub = k[:, :nj * CMP_STRIDE].reshape(b, nj, CMP_STRIDE, N_KV_HEADS, ATT_HEAD_DIM)
    hid = jnp.einsum("ld,lde->e", pe, w1)
    for r in range(nsub):
        part = jnp.einsum("bjshd,sde->bjhe", sub, w1[r * CMP_STRIDE:(r + 1) * CMP_STRIDE])
        hid = hid + part[:, r:r + nc]
    return jnp.einsum("bnhe,ed->bnhd", jax.nn.silu(hid), w2)


def _overlap_matrix(nc, ns):
    cst = np.arange(nc)[:, None] * CMP_STRIDE
    sst = np.arange(ns)[None, :] * SEL_LEN
    ov = np.clip(np.minimum(cst + CMP_LEN, sst + SEL_LEN) - np.maximum(cst, sst), 0, None)
    return jnp.asarray(ov / CMP_STRIDE, dtype=jnp.float32)


def _nsa_global(q, t, kc, vc, ks, vs):
    b, nq = q.shape[0], q.shape[1]
    nc = kc.shape[1]
    cmp_end = jnp.arange(nc) * CMP_STRIDE + (CMP_LEN - 1)
    mask_c = (cmp_end[None, :] <= t[:, None])[None, :, None, None, :]
    p_c = _masked_softmax(jnp.einsum("bqhgd,bnhd->bqhgn", q, kc) * ATT_SCALE, mask_c)
    o_c = jnp.einsum("bqhgn,bnhd->bqhgd", p_c.astype(vc.dtype), vc)
    length = ks.shape[1]
    ns = -(-length // SEL_LEN)
    pad = ((0, 0), (0, ns * SEL_LEN - length), (0, 0), (0, 0))
    ks = jnp.pad(ks, pad)
    vs = jnp.pad(vs, pad)
    imp = jnp.einsum("bqhn,nj->bqhj", jnp.sum(p_c, axis=3), _overlap_matrix(nc, ns))
    blk = jnp.arange(ns)[None, :]
    cur = (t // SEL_LEN)[:, None]
    valid = blk * SEL_LEN <= t[:, None]
    forced = (blk == 0) | (blk == cur) | (blk == cur - 1)
    imp = jnp.where(valid[None, :, None, :], jnp.where(forced[None, :, None, :], BIG, imp), -BIG)
    n_top = min(SEL_TOPK, ns)
    _, idx = lax.top_k(imp, n_top)
    kb = ks.reshape(b, ns, SEL_LEN, N_KV_HEADS, ATT_HEAD_DIM).transpose(0, 3, 1, 2, 4)
    vb = vs.reshape(b, ns, SEL_LEN, N_KV_HEADS, ATT_HEAD_DIM).transpose(0, 3, 1, 2, 4)
    bi = jnp.arange(b)[:, None, None, None]
    hi = jnp.arange(N_KV_HEADS)[None, None, :, None]
    m = n_top * SEL_LEN
    kg = kb[bi, hi, idx].reshape(b, nq, N_KV_HEADS, m, ATT_HEAD_DIM)
    vg = vb[bi, hi, idx].reshape(b, nq, N_KV_HEADS, m, ATT_HEAD_DIM)
    kpos = (idx[..., None] * SEL_LEN + jnp.arange(SEL_LEN)).reshape(b, nq, N_KV_HEADS, m)
    mask_s = (kpos <= t[None, :, None, None])[:, :, :, None, :]
    p_s = _masked_softmax(jnp.einsum("bqhgd,bqhmd->bqhgm", q, kg) * ATT_SCALE, mask_s)
    o_s = jnp.einsum("bqhgm,bqhmd->bqhgd", p_s.astype(vg.dtype), vg)
    return o_c, o_s


def _nsa_window(q, t, kw, vw, kpos):
    d = t[:, None] - kpos[None, :]
    mask = ((d >= 0) & (d < WINDOW) & (kpos >= 0)[None, :])[None, :, None, None, :]
    p = _masked_softmax(jnp.einsum("bqhgd,bkhd->bqhgk", q, kw) * ATT_SCALE, mask)
    return jnp.einsum("bqhgk,bkhd->bqhgd", p.astype(vw.dtype), vw)


def _nsa_merge(o_c, o_s, o_w, gates):
    b, t = gates.shape[0], gates.shape[1]
    g = jax.nn.sigmoid(gates).reshape(b, t, N_KV_HEADS, Q_PER_KV, 3)
    o = g[..., 0:1] * o_c + g[..., 1:2] * o_s + g[..., 2:3] * o_w
    return o.reshape(b, t, ATT_WIDTH)


def _nsa_prompt(q, kv, gates, p):
    b, t = q.shape[0], q.shape[1]
    kc = _compress(kv[:, :, 0], p["cmp_pe_k"], p["cmp_w1_k"], p["cmp_w2_k"])
    vc = _compress(kv[:, :, 1], p["cmp_pe_v"], p["cmp_w1_v"], p["cmp_w2_v"])
    ks = kv[:, :, 2]
    vs = kv[:, :, 3]
    pad_w = ((0, 0), (WINDOW, 0), (0, 0), (0, 0))
    kw = jnp.pad(kv[:, :, 4], pad_w)
    vw = jnp.pad(kv[:, :, 5], pad_w)

    def one_block(i):
        q0 = i * Q_BLOCK
        qb = lax.dynamic_slice_in_dim(q, q0, Q_BLOCK, axis=1)
        gb = lax.dynamic_slice_in_dim(gates, q0, Q_BLOCK, axis=1)
        tq = q0 + jnp.arange(Q_BLOCK, dtype=jnp.int32)
        o_c, o_s = _nsa_global(qb, tq, kc, vc, ks, vs)
        kpos = q0 - WINDOW + jnp.arange(WINDOW + Q_BLOCK, dtype=jnp.int32)
        kwb = lax.dynamic_slice_in_dim(kw, q0, WINDOW + Q_BLOCK, axis=1)
        vwb = lax.dynamic_slice_in_dim(vw, q0, WINDOW + Q_BLOCK, axis=1)
        o_w = _nsa_window(qb, tq, kwb, vwb, kpos)
        return _nsa_merge(o_c, o_s, o_w, gb)

    out = lax.map(one_block, jnp.arange(t // Q_BLOCK, dtype=jnp.int32))
    out = jnp.moveaxis(out, 0, 1).reshape(b, t, ATT_WIDTH)
    win_buf = min(WINDOW, PAST_LEN)
    win_state = jnp.stack([kw[:, kw.shape[1] - win_buf:], vw[:, vw.shape[1] - win_buf:]], axis=2)
    return out, kv[:, :, :N_KV_STREAMS], win_state


def _nsa_sample(q, kv, gates, cache_nsa_kv, page_table, cache_win_kv, p):
    b, t = q.shape[0], q.shape[1]
    n_pages = page_table.shape[1]
    past_len = n_pages * PAGE_SIZE
    past = cache_nsa_kv[page_table].reshape(b, past_len, N_KV_STREAMS, N_KV_HEADS, ATT_HEAD_DIM)
    kv_rows = kv[:, :, :N_KV_STREAMS]
    full = jnp.concatenate([past.astype(kv.dtype), kv_rows], axis=1)
    kc = _compress(full[:, :, 0], p["cmp_pe_k"], p["cmp_w1_k"], p["cmp_w2_k"])
    vc = _compress(full[:, :, 1], p["cmp_pe_v"], p["cmp_w1_v"], p["cmp_w2_v"])
    tq = past_len + jnp.arange(t, dtype=jnp.int32)
    o_c, o_s = _nsa_global(q, tq, kc, vc, full[:, :, 2], full[:, :, 3])
    win_buf = cache_win_kv.shape[1]
    win = jnp.concatenate([cache_win_kv.astype(kv.dtype), kv[:, :, 4:]], axis=1)
    kpos = past_len - win_buf + jnp.arange(win_buf + t, dtype=jnp.int32)
    o_w = _nsa_window(q, tq, win[:, :, 0], win[:, :, 1], kpos)
    return _nsa_merge(o_c, o_s, o_w, gates), kv_rows, win[:, t:]


def _layer(x, c, pos, conv_buf, h0, ffn_buf, nsa_fn, p):
    mod = jnp.einsum("bd,de->be", jax.nn.silu(c), p["ada_w"]) + p["ada_b"]
    sh1, sc1, g1, sh2, sc2, g2 = jnp.split(mod[:, None, :], 6, axis=-1)
    h = _rmsnorm(x, p["norm1_w"]) * (1.0 + sc1) + sh1
    proj = h @ p["w_in"]
    z, xbc, dt_raw, q, kv, att_g, merge_g = jnp.split(proj, np.cumsum(IN_SIZES)[:-1].tolist(), axis=-1)
    b, t = x.shape[0], x.shape[1]
    y_ssm, conv_new, h_new = _mamba(z, xbc, dt_raw, conv_buf, h0, p)
    q = _rope(q.reshape(b, t, N_ATT_HEADS, ATT_HEAD_DIM), pos).reshape(b, t, N_KV_HEADS, Q_PER_KV, ATT_HEAD_DIM)
    kv = kv.reshape(b, t, N_KV_PROJ, N_KV_HEADS, ATT_HEAD_DIM)
    kv = kv.at[:, :, 0::2].set(_rope(kv[:, :, 0::2], pos))
    y_att, kv_rows, win_state = nsa_fn(q, kv, att_g)
    gm = jax.nn.sigmoid(merge_g)
    merged = gm[..., :D_MODEL] * (y_ssm @ p["w_ssm_out"]) + gm[..., D_MODEL:] * (y_att @ p["w_att_out"])
    x = x + g1 * (merged @ p["w_out"])
    h2 = _rmsnorm(x, p["norm2_w"]) * (1.0 + sc2) + sh2
    u = h2 @ p["ffn_w_up"]
    up = jnp.concatenate([ffn_buf.astype(u.dtype), u], axis=1)
    ffn_new = up[:, up.shape[1] - (FFN_CONV - 1):]
    u = _causal_dwconv(up, p["ffn_conv_w"], p["ffn_conv_b"])
    f = (jax.nn.silu(u[..., :D_FF]) * u[..., D_FF:]) @ p["ffn_w_down"]
    x = x + g2 * f
    return x, kv_rows, win_state, conv_new, h_new, ffn_new


def setup_inputs(seed: int = 0) -> dict:
    key = jax.random.key(seed)
    k = jax.random.split(key, 40)
    f32 = jnp.float32

    def nrm(kk, shape, scale):
        return jax.random.normal(kk, shape, f32) * scale

    n_pages = PAST_LEN // PAGE_SIZE
    n_used = DEC_BATCH * n_pages
    n_pool = (n_used * 5 + 3) // 4
    win_buf = min(WINDOW, PAST_LEN)
    page_table = jax.random.permutation(k[5], n_pool)[:n_used].reshape(DEC_BATCH, n_pages).astype(jnp.int32)
    dt_col = D_INNER + CONV_DIM
    w_in = nrm(k[10], (D_MODEL, D_IN_PROJ), D_MODEL ** -0.5)
    w_in = w_in.at[:, dt_col:dt_col + N_SSM_HEADS].multiply(0.1)
    dt0 = jnp.exp(jax.random.uniform(k[12], (N_SSM_HEADS,), f32, math.log(1e-3), math.log(1e-1)))
    return {
        "x_prompt": nrm(k[0], (BATCH, SEQ, D_MODEL), 1.0),
        "x_sample": nrm(k[1], (DEC_BATCH, DEC_SEQ, D_MODEL), 1.0),
        "c_prompt": nrm(k[2], (BATCH, D_MODEL), 1.0),
        "c_sample": nrm(k[3], (DEC_BATCH, D_MODEL), 1.0),
        "cache_nsa_kv": nrm(k[4], (n_pool, PAGE_SIZE, N_KV_STREAMS, N_KV_HEADS, ATT_HEAD_DIM), 1.0),
        "page_table": page_table,
        "cache_win_kv": nrm(k[6], (DEC_BATCH, win_buf, 2, N_KV_HEADS, ATT_HEAD_DIM), 1.0),
        "state_ssm": nrm(k[7], (DEC_BATCH, N_SSM_HEADS, SSM_HEAD_DIM, SSM_STATE), 0.3),
        "state_ssm_conv": nrm(k[8], (DEC_BATCH, SSM_CONV - 1, CONV_DIM), 1.0),
        "state_ffn_conv": nrm(k[9], (DEC_BATCH, FFN_CONV - 1, 2 * D_FF), 1.0),
        "ada_w": nrm(k[11], (D_MODEL, 6 * D_MODEL), 0.5 * D_MODEL ** -0.5),
        "ada_b": nrm(k[13], (6 * D_MODEL,), 0.01),
        "norm1_w": 1.0 + nrm(k[14], (D_MODEL,), 0.1),
        "norm2_w": 1.0 + nrm(k[15], (D_MODEL,), 0.1),
        "final_norm_w": 1.0 + nrm(k[16], (D_MODEL,), 0.1),
        "w_in": w_in,
        "ssm_conv_w": nrm(k[17], (SSM_CONV, CONV_DIM), 0.5),
        "ssm_conv_b": nrm(k[18], (CONV_DIM,), 0.01),
        "ssm_dt_bias": dt0 + jnp.log(-jnp.expm1(-dt0)),
        "ssm_A_log": jnp.log(jax.random.uniform(k[19], (N_SSM_HEADS,), f32, 1.0, 16.0)),
        "ssm_D": 1.0 + nrm(k[20], (N_SSM_HEADS,), 0.1),
        "ssm_norm_w": 1.0 + nrm(k[21], (D_INNER,), 0.1),
        "cmp_pe_k": nrm(k[22], (CMP_LEN, ATT_HEAD_DIM), 0.5),
        "cmp_w1_k": nrm(k[23], (CMP_LEN, ATT_HEAD_DIM, CMP_HIDDEN), (CMP_LEN * ATT_HEAD_DIM) ** -0.5),
        "cmp_w2_k": nrm(k[24], (CMP_HIDDEN, ATT_HEAD_DIM), 2.0 * CMP_HIDDEN ** -0.5),
        "cmp_pe_v": nrm(k[25], (CMP_LEN, ATT_HEAD_DIM), 0.5),
        "cmp_w1_v": nrm(k[26], (CMP_LEN, ATT_HEAD_DIM, CMP_HIDDEN), (CMP_LEN * ATT_HEAD_DIM) ** -0.5),
        "cmp_w2_v": nrm(k[27], (CMP_HIDDEN, ATT_HEAD_DIM), 2.0 * CMP_HIDDEN ** -0.5),
        "w_ssm_out": nrm(k[28], (D_INNER, D_MODEL), D_INNER ** -0.5),
        "w_att_out": nrm(k[29], (ATT_WIDTH, D_MODEL), ATT_WIDTH ** -0.5),
        "w_out": nrm(k[30], (D_MODEL, D_MODEL), D_MODEL ** -0.5),
        "ffn_w_up": nrm(k[31], (D_MODEL, 2 * D_FF), D_MODEL ** -0.5),
        "ffn_conv_w": nrm(k[32], (FFN_CONV, 2 * D_FF), 0.6),
        "ffn_conv_b": nrm(k[33], (2 * D_FF,), 0.01),
        "ffn_w_down": nrm(k[34], (D_FF, D_MODEL), D_FF ** -0.5),
    }


def reference(x_prompt, x_sample, c_prompt, c_sample, cache_nsa_kv, page_table, cache_win_kv, state_ssm,
              state_ssm_conv, state_ffn_conv, ada_w, ada_b, norm1_w, norm2_w, final_norm_w, w_in,
              ssm_conv_w, ssm_conv_b, ssm_dt_bias, ssm_A_log, ssm_D, ssm_norm_w,
              cmp_pe_k, cmp_w1_k, cmp_w2_k, cmp_pe_v, cmp_w1_v, cmp_w2_v,
              w_ssm_out, w_att_out, w_out, ffn_w_up, ffn_conv_w, ffn_conv_b, ffn_w_down):
    p = {
        "ada_w": ada_w, "ada_b": ada_b, "norm1_w": norm1_w, "norm2_w": norm2_w, "w_in": w_in,
        "ssm_conv_w": ssm_conv_w, "ssm_conv_b": ssm_conv_b, "ssm_dt_bias": ssm_dt_bias,
        "ssm_A_log": ssm_A_log, "ssm_D": ssm_D, "ssm_norm_w": ssm_norm_w,
        "cmp_pe_k": cmp_pe_k, "cmp_w1_k": cmp_w1_k, "cmp_w2_k": cmp_w2_k,
        "cmp_pe_v": cmp_pe_v, "cmp_w1_v": cmp_w1_v, "cmp_w2_v": cmp_w2_v,
        "w_ssm_out": w_ssm_out, "w_att_out": w_att_out, "w_out": w_out,
        "ffn_w_up": ffn_w_up, "ffn_conv_w": ffn_conv_w, "ffn_conv_b": ffn_conv_b, "ffn_w_down": ffn_w_down,
    }
    b, t = x_prompt.shape[0], x_prompt.shape[1]
    ds = x_sample.shape[1]
    past_len = page_table.shape[1] * PAGE_SIZE
    pos_p = jnp.arange(t, dtype=jnp.int32)
    pos_s = past_len + jnp.arange(ds, dtype=jnp.int32)
    dtype = x_prompt.dtype
    hp = x_prompt
    hs = x_sample
    for _ in range(DEPTH):
        hp, kv_p, win_p, conv_p, ssm_p, ffn_p = _layer(
            hp, c_prompt, pos_p,
            jnp.zeros((b, SSM_CONV - 1, CONV_DIM), dtype),
            jnp.zeros((b, N_SSM_HEADS, SSM_HEAD_DIM, SSM_STATE), dtype),
            jnp.zeros((b, FFN_CONV - 1, 2 * D_FF), dtype),
            lambda qq, kk, gg: _nsa_prompt(qq, kk, gg, p), p)
        hs, kv_s, win_s, conv_s, ssm_s, ffn_s = _layer(
            hs, c_sample, pos_s, state_ssm_conv, state_ssm, state_ffn_conv,
            lambda qq, kk, gg: _nsa_sample(qq, kk, gg, cache_nsa_kv, page_table, cache_win_kv, p), p)
    y_prompt = _rmsnorm(hp, final_norm_w)
    y_sample = _rmsnorm(hs, final_norm_w)
    return (y_prompt, y_sample, kv_p, kv_s, win_p, win_s, ssm_p, ssm_s, conv_p, conv_s, ffn_p, ffn_s)
```

```python
from contextlib import ExitStack
import numpy as np
import concourse.bass as bass
import concourse.mybir as mybir

F32 = mybir.dt.float32
BF16 = mybir.dt.bfloat16
I32 = mybir.dt.int32
ALU = mybir.AluOpType
AF = mybir.ActivationFunctionType
AX = mybir.AxisListType


class Res:
    __slots__ = ("name", "w", "r", "multi")

    def __init__(self, name="", multi=False):
        self.name = name
        self.w = {}
        self.r = {}
        self.multi = multi


class Ctx:
    def __init__(self, nc, es):
        self.nc = nc
        self.es = es
        self.engs = {"pe": nc.tensor, "act": nc.scalar, "dve": nc.vector, "pool": nc.gpsimd, "sp": nc.sync}
        self.sem = {}
        self.cnt = {}
        for e in ("pe", "act", "dve", "pool"):
            self.sem[e] = es.enter_context(nc.semaphore("s_" + e))
            self.cnt[e] = 0
        self.known = {e: {} for e in self.engs}
        self.nsem = 4
        self.uid = 0
        self.psum_tiles = []
        self.psum_i = 0
        self.rot = list(range(8))
        self.free_sems = []
        self.stage_sems = []

    def sb(self, shape, dtype, name=None, es=None):
        self.uid += 1
        name = (name or "t") + "_%d" % self.uid
        t = (es or self.es).enter_context(self.nc.sbuf_tensor(name, list(shape), dtype))
        return t, Res(name)

    def dram(self, shape, dtype, name):
        t = self.nc.dram_tensor(name, list(shape), dtype, kind="Internal")
        return t.ap(), Res(name, multi=True)

    def dsem(self, name, persist=False):
        if self.free_sems and not persist:
            k = self.free_sems.pop()
            self.stage_sems.append(k)
            return k
        self.nsem += 1
        self.uid += 1
        k = "d_%s_%d" % (name, self.uid)
        self.sem[k] = self.es.enter_context(self.nc.semaphore(k))
        self.cnt[k] = 0
        if not persist:
            self.stage_sems.append(k)
        return k

    def init_psum(self):
        for i in range(8):
            t = self.es.enter_context(self.nc.psum_tensor("ps%d" % i, [128, 512], F32))
            self.psum_tiles.append((t, Res("ps%d" % i)))

    def psum(self):
        t = self.psum_tiles[self.rot[self.psum_i % len(self.rot)]]
        self.psum_i += 1
        return t

    def psum_fixed(self, i):
        return self.psum_tiles[i]

    def _wait(self, eng, reads, writes):
        need = {}
        for r in reads:
            for s, v in r.w.items():
                if need.get(s, 0) < v:
                    need[s] = v
        for w in writes:
            if w.multi:
                continue
            for s, v in w.w.items():
                if need.get(s, 0) < v:
                    need[s] = v
            for s, v in w.r.items():
                if need.get(s, 0) < v:
                    need[s] = v
        kn = self.known[eng]
        for s, v in need.items():
            if kn.get(s, 0) < v:
                self.engs[eng].wait_ge(self.sem[s], v)
                kn[s] = v

    def _mark(self, ev, reads, writes):
        s, v = ev
        for w in writes:
            if w.multi:
                if w.w.get(s, 0) < v:
                    w.w[s] = v
                continue
            w.w = {s: v}
            w.r = {}
        for r in reads:
            if r.r.get(s, 0) < v:
                r.r[s] = v

    def op(self, eng, fn, reads=(), writes=()):
        self._wait(eng, reads, writes)
        inst = fn(self.engs[eng])
        self.cnt[eng] += 1
        inst.then_inc(self.sem[eng], 1)
        self._mark((eng, self.cnt[eng]), reads, writes)

    def dma(self, q, out, in_, reads, writes, sem, **kw):
        self._wait(q, reads, writes)
        self.cnt[sem] += 16
        self.engs[q].dma_start(out=out, in_=in_, **kw).then_inc(self.sem[sem], 16)
        self._mark((sem, self.cnt[sem]), reads, writes)

    def barrier(self):
        for eng in self.engs:
            kn = self.known[eng]
            for s, v in self.cnt.items():
                if v > 0 and kn.get(s, 0) < v:
                    self.engs[eng].wait_ge(self.sem[s], v)
                    kn[s] = v
        self.free_sems.extend(self.stage_sems)
        self.stage_sems = []

    def wait_all(self, eng, resources):
        self._wait(eng, resources, ())


from concourse.bass_utils import run_bass_kernel_spmd

D = 1024
NT = 32
F0T = 14
NFT = NT - F0T
TV = NT * 128
TF = NFT * 128
F0 = F0T * 128
NS = 16
DIN = 9808
CA0, CA1 = 5120, 7760
WA = CA1 - CA0
A_DT, A_Q, A_KV, A_G = 0, 32, 1056, 2592
EPS = 1e-6
SCALE = 0.125


def rr(n):
    i = 0
    while True:
        yield i % n
        i += 1


import os
STOP = 99
KX = 0


def build_program():
    nc = bass.Bass("TRN2", target_bir_lowering=False)

    def din(name, shape, dt=F32):
        return nc.dram_tensor(name, list(shape), dt, kind="ExternalInput").ap()

    def dout(name, shape):
        return nc.dram_tensor(name, list(shape), F32, kind="ExternalOutput").ap()

    xv = din("xv", [TV, D]); xs = din("xs", [NS, D]); cin = din("cin", [17, D])
    ada_w = din("ada_w", [D, 6 * D]); ada_b = din("ada_b", [6 * D])
    norm1_w = din("norm1_w", [D]); norm2_w = din("norm2_w", [D]); final_w = din("final_norm_w", [D])
    w_in = din("w_in", [D, DIN])
    ssm_conv_w = din("ssm_conv_w", [4, 3072]); ssm_conv_b = din("ssm_conv_b", [3072])
    st_conv = din("st_conv", [NS, 3, 3072])
    csv = din("csv", [TV, 16]); css = din("css", [NS, 16])
    pfd = din("pf", [128, 1])
    cwin = din("cwin", [NS, 512, 512])
    dt_bias = din("ssm_dt_bias", [32]); A_log = din("ssm_A_log", [32]); ssm_Dv = din("ssm_D", [32])
    ssm_nw = din("ssm_norm_w", [2048])
    st_ssm = din("st_ssm", [NS, 32, 64, 128])
    selBG = din("selBG", [NS, NS * 4 * 128])
    w1k_d = din("cmp_w1_k", [32, 64, 128]); w1v_d = din("cmp_w1_v", [32, 64, 128])
    w2k_d = din("cmp_w2_k", [128, 64]); w2v_d = din("cmp_w2_v", [128, 64])
    pek_d = din("cmp_pe_k", [32, 64]); pev_d = din("cmp_pe_v", [32, 64])
    ov_d = din("ovm", [256, 64])
    wso_d = din("w_ssm_out", [2048, 1024]); wao_d = din("w_att_out", [1024, 1024]); wout_d = din("w_out", [1024, 1024])
    wup_d = din("ffn_w_up", [1024, 5632]); wdn_d = din("ffn_w_down", [2816, 1024])
    fcw_d = din("ffn_conv_w", [3, 5632]); fcb_d = din("ffn_conv_b", [5632])
    st_ffn = din("st_ffn", [NS, 2, 5632])
    maskC_d = din("maskC", [NFT, 2, 128, 128])
    selc_d = din("selc", [NFT, 128, 3, 64])
    cache_d = din("cache_nsa_kv", [2560 * 128, 1024])
    ptab_d = din("ptab", [NS * 16], I32)
    iotap_d = din("iotap", [128, 1])
    sconst_d = din("sconst", [1, 3 * 64])
    smask_d = din("smask", [128, 3])
    ovs_d = din("ovs", [128, 65])

    o_kvp = dout("o_kvp", [2048, 1024]); o_kvs = dout("o_kvs", [NS, 1024])
    o_winp = dout("o_winp", [512, 512]); o_wins = dout("o_wins", [NS, 512, 512])
    o_convp = dout("o_convp", [3, 3072]); o_convs = dout("o_convs", [NS, 3, 3072])
    o_ssmp = dout("o_ssmp", [2048, 128]); o_ssms = dout("o_ssms", [NS, 32, 64, 128])
    o_yp = dout("o_yp", [2048, 1024]); o_ys = dout("o_ys", [NS, 1024])
    o_ffnp = dout("o_ffnp", [2, 5632]); o_ffns = dout("o_ffns", [NS, 2, 5632])

    class _Stop(Exception):
        pass

    try:
      with ExitStack() as es:
        c = Ctx(nc, es)
        c.init_psum()
        outs_res = []

        DBG = False

        def dbg(name, ap, shape, reads):
            if not DBG:
                return
            o = nc.dram_tensor("dbg_" + name, list(shape), F32, kind="ExternalOutput").ap()
            q = c.dsem("dbg")
            c.dma("pool" if ap.dtype != F32 else "sp", o, ap, reads, [], q)
            ev = Res(); ev.w = {q: c.cnt[q]}; outs_res.append(ev)

        def chk(n):
            if STOP < n:
                c.wait_all("sp", outs_res)
                raise _Stop()

        TA, TAr = c.dram([TV, WA], F32, "TA")
        Zs, Zr = c.dram([TF, 2048], F32, "Zs")
        XCT, XCTr = c.dram([3072, TV], F32, "XCT")
        GMT, GMTr = c.dram([2048, TF], F32, "GMT")
        PS, PSr = c.dram([NS, DIN], F32, "PSs")
        YNT, YNTr = c.dram([2048, TF], BF16, "YNT")
        YAT, YATr = c.dram([1024, TF], BF16, "YAT")
        KVS, KVSr = c.dram([NS, 1536], F32, "KVS")
        SG, SGr = c.dram([NS, 48], F32, "SG")
        MODS, MODSr = c.dram([17, 6 * D], F32, "MODS")
        X1, X1r = c.dram([TF, D], F32, "X1")
        X1s, X1sr = c.dram([NS, D], F32, "X1s")
        US, USr = c.dram([NS, 5632], F32, "US")

        ident, identr = c.sb([128, 128], BF16, "ident")
        c.op("pool", lambda e: e.memset(ident[:], 1.0), [], [identr])
        c.op("pool", lambda e: e.affine_select(out=ident[:], in_=ident[:], pattern=[[-1, 128]],
                                               compare_op=ALU.is_equal, fill=0.0, base=0, channel_multiplier=1),
             [identr], [identr])
        identf, identfr = c.sb([128, 128], F32, "identf")
        c.op("dve", lambda e: e.tensor_copy(out=identf[:], in_=ident[:]), [identr], [identfr])
        vstg = [c.sb([128, 128], F32, "vstg") for _ in range(2)]
        vsem = [c.dsem("vstg", persist=True) for _ in range(2)]
        vq = rr(2)

        def load_vecT(dst_ap, src2d, T, dst_r):
            j = next(vq)
            vt, vr = vstg[j]
            c.dma("sp", vt[0:T, :], src2d, [], [vr], vsem[j])
            ps, psr = c.psum()
            c.op("pe", lambda e: e.transpose(out=ps[:, 0:T], in_=vt[0:T, :], identity=identf[0:T, 0:T]), [vr, identfr], [psr])
            c.op("dve", lambda e: e.tensor_copy(out=dst_ap, in_=ps[:, 0:T]), [psr], [dst_r])

        modT, modr = c.sb([128, 48, 17], F32, "modT")
        pf, pfr = c.sb([128, 1], F32, "pf")
        sm = c.dsem("small", persist=True)
        c.dma("sp", pf[:], pfd, [], [pfr], sm)
        n1w, n1r = c.sb([128, 8], F32, "n1w")
        load_vecT(n1w[:], norm1_w.rearrange("(t p) -> t p", p=128), 8, n1r)
        adab, adabr = c.sb([128, 48], F32, "adab")
        load_vecT(adab[:], ada_b.rearrange("(t p) -> t p", p=128), 48, adabr)
        a1, a1r = c.sb([128, 8, 17], F32, "a1")
        yaTs_p, yaTs_pr = c.sb([128, 8, NS], BF16, "yaTs")
        evq = rr(2)

        def evac(out_ap, in_ap, reads, writes):
            if next(evq) == 0:
                c.op("act", lambda e: e.copy(out=out_ap, in_=in_ap), reads, writes)
            else:
                c.op("dve", lambda e: e.tensor_copy(out=out_ap, in_=in_ap), reads, writes)

        def mm8(ps_ap, lhs_fn, rhs_fn, reads, writes, n=8):
            def f(e):
                for kt in range(n):
                    ins = e.matmul(ps_ap, lhsT=lhs_fn(kt), rhs=rhs_fn(kt), start=(kt == 0), stop=(kt == n - 1))
                return ins
            c.op("pe", f, reads, writes)

        with ExitStack() as s0:
            csb, csr = c.sb([17, D], F32, "csb", s0)
            cbf, cbr = c.sb([17, D], BF16, "cbf", s0)
            cT, cTr = c.sb([128, 8, 17], BF16, "cT", s0)
            c.dma("sp", csb[:], cin, [], [csr], c.dsem("cin"))
            c.op("act", lambda e: e.activation(out=cbf[:], in_=csb[:], func=AF.Silu), [csr], [cbr])
            pt, pr = c.psum()
            ptb = pt[:].bitcast(BF16)

            def f(e):
                for kt in range(8):
                    ins = e.transpose(out=ptb[:, kt * 32:kt * 32 + 17], in_=cbf[0:17, kt * 128:(kt + 1) * 128],
                                      identity=ident[0:17, 0:17])
                return ins
            c.op("pe", f, [cbr, identr], [pr])
            c.op("dve", lambda e: e.tensor_copy(out=cT[:], in_=ptb[:, 0:256].rearrange("p (k c) -> p k c", c=32)[:, :, 0:17]),
                 [pr], [cTr])
            wb = [c.sb([128, 8, 512], BF16, "adaw", s0) for _ in range(2)]
            modtm = [c.sb([17, 512], F32, "modtm", s0) for _ in range(2)]
            modsem = [c.dsem("modtm") for _ in range(2)]
            adabt, adabtr = c.sb([17, 6 * D], F32, "adabt", s0)
            c.dma("sp", adabt[:], ada_b.partition_broadcast(17), [], [adabtr], c.dsem("adabt"))
            wsem = [c.dsem("adaw") for _ in range(2)]
            for k in range(12):
                wt, wr = wb[k % 2]
                c.dma("pool", wt[:], ada_w[:, k * 512:(k + 1) * 512].rearrange("(kt p) c -> p kt c", p=128),
                      [], [wr], wsem[k % 2])
                ps, psr = c.psum()

                def f(e, wt=wt, ps=ps):
                    for et in range(4):
                        for kt in range(8):
                            ins = e.matmul(ps[:, et * 32:et * 32 + 17], lhsT=wt[:, kt, et * 128:(et + 1) * 128],
                                           rhs=cT[:, kt, :], start=(kt == 0), stop=(kt == 7))
                    return ins
                c.op("pe", f, [wr, cTr], [psr])
                ps2, ps2r = c.psum()

                def f2(e, wt=wt, ps2=ps2):
                    for kt in range(8):
                        ins = e.matmul(ps2[0:17, :], lhsT=cT[:, kt, :], rhs=wt[:, kt, :], start=(kt == 0), stop=(kt == 7))
                    return ins
                c.op("pe", f2, [wr, cTr], [ps2r])
                mt_, mtr_ = modtm[k % 2]
                c.op("dve", lambda e, ps2=ps2, mt_=mt_, k=k: e.tensor_tensor(out=mt_[:], in0=ps2[0:17, :],
                                                                            in1=adabt[:, k * 512:(k + 1) * 512], op=ALU.add),
                     [ps2r, adabtr], [mtr_])
                c.dma("sp", MODS[:, k * 512:(k + 1) * 512], mt_[:], [mtr_], [MODSr], modsem[k % 2])
                c.op("dve", lambda e, ps=ps, k=k: e.tensor_tensor(
                    out=modT[:, 4 * k:4 * k + 4, :],
                    in0=ps[:, 0:128].rearrange("p (a b) -> p a b", b=32)[:, :, 0:17],
                    in1=adab[:, 4 * k:4 * k + 4].unsqueeze(2).to_broadcast([128, 4, 17]), op=ALU.add),
                    [psr, adabr], [modr])
            c.op("dve", lambda e: e.tensor_scalar(out=a1[:], in0=modT[:, 8:16, :], scalar1=1.0, scalar2=None, op0=ALU.add),
                 [modr], [a1r])
            c.op("dve", lambda e: e.tensor_tensor(out=a1[:], in0=a1[:], in1=n1w[:].unsqueeze(2).to_broadcast([128, 8, 17]),
                                                  op=ALU.mult), [a1r, n1r], [a1r])

        c.barrier()
        chk(1)
        sP12 = ExitStack()
        es.enter_context(sP12)
        hT, hTr_ = c.sb([128, 8, TV], BF16, "hT", sP12)
        hTres = [Res("hT%d" % i) for i in range(NT)]
        hTs, hTsr = c.sb([128, 8, NS], BF16, "hTs", sP12)

        with ExitStack() as s1:
            xb = [c.sb([128, D], F32, "xb", s1) for _ in range(2)]
            xsem = [c.dsem("xb") for _ in range(2)]
            junk, junkr = c.sb([128, D], BF16, "junk", s1)
            xn = [c.sb([128, D], BF16, "xn", s1) for _ in range(2)]
            stt = [c.sb([128, 4], F32, "stt", s1) for _ in range(2)]
            tmpf = [c.sb([128, 8, 128], F32, "tmpf", s1) for _ in range(2)]
            for i in range(NT + 1):
                rows = 128 if i < NT else NS
                xt, xr = xb[i % 2]
                src = xv[i * 128:(i + 1) * 128, :] if i < NT else xs
                c.dma("sp", xt[0:rows, :], src, [], [xr], xsem[i % 2])
                st, str_ = stt[i % 2]
                xnt, xnr = xn[i % 2]
                c.op("act", lambda e, xt=xt, st=st, rows=rows: e.activation(out=junk[0:rows, :], in_=xt[0:rows, :], func=AF.Square,
                                                                            accum_out=st[0:rows, 0:1]), [xr], [junkr, str_])
                c.op("act", lambda e, st=st, rows=rows: e.activation(out=st[0:rows, 1:2], in_=st[0:rows, 0:1], func=AF.Sqrt,
                                                                     scale=1.0 / D, bias=EPS), [str_], [str_])
                c.op("dve", lambda e, st=st, rows=rows: e.reciprocal(out=st[0:rows, 2:3], in_=st[0:rows, 1:2]), [str_], [str_])
                c.op("dve", lambda e, xt=xt, st=st, xnt=xnt, rows=rows: e.tensor_scalar(
                    out=xnt[0:rows, :], in0=xt[0:rows, :], scalar1=st[0:rows, 2:3], scalar2=None, op0=ALU.mult), [xr, str_], [xnr])
                pt, pr = c.psum()
                ptb = pt[:].bitcast(BF16)

                def f(e, xnt=xnt, ptb=ptb, rows=rows):
                    for kt in range(8):
                        ins = e.transpose(out=ptb[:, kt * 128:kt * 128 + rows], in_=xnt[0:rows, kt * 128:(kt + 1) * 128],
                                          identity=ident[0:rows, 0:rows])
                    return ins
                c.op("pe", f, [xnr, identr], [pr])
                tf_, tfr = tmpf[i % 2]
                pv = ptb.rearrange("p (k c) -> p k c", c=128)
                if i < NT:
                    c.op("dve", lambda e, tf_=tf_, pv=pv: e.tensor_tensor(
                        out=tf_[:], in0=pv, in1=a1[:, :, 0:1].to_broadcast([128, 8, 128]), op=ALU.mult), [pr, a1r], [tfr])
                    c.op("pool", lambda e, tf_=tf_, i=i: e.tensor_tensor(
                        out=hT[:, :, i * 128:(i + 1) * 128], in0=tf_[:], in1=modT[:, 0:8, 0:1].to_broadcast([128, 8, 128]),
                        op=ALU.add), [tfr, modr], [hTres[i]])
                else:
                    c.op("dve", lambda e, tf_=tf_, pv=pv: e.tensor_tensor(
                        out=tf_[:, :, 0:NS], in0=pv[:, :, 0:NS], in1=a1[:, :, 1:17], op=ALU.mult), [pr, a1r], [tfr])
                    c.op("pool", lambda e, tf_=tf_: e.tensor_tensor(
                        out=hTs[:], in0=tf_[:, :, 0:NS], in1=modT[:, 0:8, 1:17], op=ALU.add), [tfr, modr], [hTsr])

        c.barrier()
        chk(2)
        with ExitStack() as s2:
            wb = [c.sb([128, 8, 512], BF16, "win", s2) for _ in range(2)]
            wsem = [c.dsem("win") for _ in range(2)]
            stg = [c.sb([128, 512], F32, "stg", s2) for _ in range(4)]
            ssem = [c.dsem("stg") for _ in range(4)]
            sq = rr(4)
            wq = rr(2)

            def tokmajor(col0, cw, tiles, dst, dst_r, dcol0, row_of_tile, sample_col0):
                k = next(wq)
                wt, wr = wb[k]
                c.dma("pool", wt[:, :, 0:cw], w_in[:, col0:col0 + cw].rearrange("(kt p) c -> p kt c", p=128), [], [wr], wsem[k])
                for i in list(tiles) + [NT]:
                    rows = 128 if i < NT else NS
                    ps, psr = c.psum()
                    if i < NT:
                        mm8(ps[:, 0:cw], lambda kt, i=i: hT[:, kt, i * 128:(i + 1) * 128], lambda kt: wt[:, kt, 0:cw],
                            [hTres[i], wr], [psr])
                    else:
                        mm8(ps[0:NS, 0:cw], lambda kt: hTs[:, kt, :], lambda kt: wt[:, kt, 0:cw], [hTsr, wr], [psr])
                    j = next(sq)
                    sg, sgr = stg[j]
                    evac(sg[0:rows, 0:cw], ps[0:rows, 0:cw], [psr], [sgr])
                    if i < NT:
                        r0 = row_of_tile(i)
                        c.dma("sp", dst[r0:r0 + 128, dcol0:dcol0 + cw], sg[:, 0:cw], [sgr], [dst_r], ssem[j])
                    else:
                        c.dma("sp", PS[:, sample_col0:sample_col0 + cw], sg[0:NS, 0:cw], [sgr], [PSr], ssem[j])

            for k in range(6):
                cw = 512 if k < 5 else WA - 2560
                tokmajor(CA0 + 512 * k, cw, range(NT), TA, TAr, 512 * k, lambda i: i * 128, CA0 + 512 * k)
            chk(3)
            for k in range(4):
                tokmajor(512 * k, 512, range(F0T, NT), Zs, Zr, 512 * k, lambda i: (i - F0T) * 128, 512 * k)

            chk(4)
            cwt, cwr = c.sb([128, 4, 24], F32, "convw", s2)
            load_vecT(cwt[:].rearrange("p k t -> p (k t)"), ssm_conv_w.rearrange("k (t p) -> (k t) p", p=128), 96, cwr)
            cbt, cbr_ = c.sb([128, 24], F32, "convb", s2)
            load_vecT(cbt[:], ssm_conv_b.rearrange("(t p) -> t p", p=128), 24, cbr_)
            wsm = [c.sb([128, 8, 512], BF16, "wsm", s2) for _ in range(2)]
            wsmsem = [c.dsem("wsm") for _ in range(2)]
            urow = [c.sb([128, 3 + TV], F32, "urow", s2) for _ in range(2)]
            acc = [c.sb([128, TV], F32, "acc", s2) for _ in range(2)]
            accsem = [c.dsem("acc") for _ in range(2)]
            usem = [c.dsem("urow") for _ in range(2)]
            for u, ur in urow:
                c.op("pool", lambda e, u=u: e.memset(u[:, 0:3], 0.0), [], [ur])
            wsq = rr(2)

            fm_state = {}

            def featmajor_tile(col0):
                base = fm_state.get("base")
                if base is None or not (base <= col0 < base + fm_state["n"]):
                    k = next(wsq)
                    wt, wr = wsm[k]
                    n = min(512, fm_state["end"] - col0)
                    c.dma("pool", wt[:, :, 0:n], w_in[:, col0:col0 + n].rearrange("(kt p) c -> p kt c", p=128), [], [wr], wsmsem[k])
                    ps, psr = c.psum()
                    mm8(ps[0:NS, 0:n], lambda kt: hTs[:, kt, :], lambda kt: wt[:, kt, 0:n], [hTsr, wr], [psr])
                    j = next(sq)
                    sg, sgr = stg[j]
                    evac(sg[0:NS, 0:n], ps[0:NS, 0:n], [psr], [sgr])
                    c.dma("sp", PS[:, col0:col0 + n], sg[0:NS, 0:n], [sgr], [PSr], ssem[j])
                    fm_state.update(base=col0, n=n, wt=wt, wr=wr)
                o = col0 - fm_state["base"]
                wt = fm_state["wt"]
                return wt[:, :, o:o + 128], fm_state["wr"]

            fm_state["end"] = 5120
            for ct in range(24):
                wt, wr = featmajor_tile(2048 + ct * 128)
                u, ur = urow[ct % 2]
                for tc in range(8):
                    ps, psr = c.psum()
                    mm8(ps[:], lambda kt: wt[:, kt, :], lambda kt, tc=tc: hT[:, kt, tc * 512:(tc + 1) * 512],
                        [wr] + hTres[tc * 4:tc * 4 + 4], [psr])
                    evac(u[:, 3 + tc * 512:3 + (tc + 1) * 512], ps[:], [psr], [ur])
                c.dma("sp", o_convp[:, ct * 128:(ct + 1) * 128].rearrange("k c -> c k"), u[:, TV:TV + 3], [ur], [], usem[ct % 2],
                      allow_slow_non_contiguous=True)
                ev = Res(); ev.w = {usem[ct % 2]: c.cnt[usem[ct % 2]]}; outs_res.append(ev)
                c.op("pool", lambda e, u=u: e.tensor_scalar(out=u[:, 3:3 + 2048], in0=u[:, 3:3 + 2048], scalar1=pf[:, 0:1],
                                                           scalar2=None, op0=ALU.mult), [ur, pfr], [ur])
                a, ar = acc[ct % 2]
                c.op("dve", lambda e, a=a, u=u, ct=ct: e.tensor_scalar(out=a[:], in0=u[:, 0:TV], scalar1=cwt[:, 0, ct:ct + 1],
                                                                      scalar2=None, op0=ALU.mult), [ur, cwr], [ar])
                c.op("dve", lambda e, a=a, u=u, ct=ct: e.scalar_tensor_tensor(out=a[:], in0=u[:, 1:TV + 1], scalar=cwt[:, 1, ct:ct + 1],
                                                                             in1=a[:], op0=ALU.mult, op1=ALU.add), [ur, cwr, ar], [ar])
                c.op("dve", lambda e, a=a, u=u, ct=ct: e.scalar_tensor_tensor(out=a[:], in0=u[:, 2:TV + 2], scalar=cwt[:, 2, ct:ct + 1],
                                                                              in1=a[:], op0=ALU.mult, op1=ALU.add), [ur, cwr, ar], [ar])
                c.op("dve", lambda e, a=a, u=u, ct=ct: e.scalar_tensor_tensor(out=a[:], in0=u[:, 3:TV + 3], scalar=cwt[:, 3, ct:ct + 1],
                                                                              in1=a[:], op0=ALU.mult, op1=ALU.add), [ur, cwr, ar], [ar])
                c.op("act", lambda e, a=a, ct=ct: e.activation(out=a[:], in_=a[:], func=AF.Silu, bias=cbt[:, ct:ct + 1]),
                     [ar, cbr_], [ar])
                c.dma("sp", XCT[ct * 128:(ct + 1) * 128, :], a[:], [ar], [XCTr], accsem[ct % 2])

            chk(5)
            fm_state["end"] = 9808
            fm_state["base"] = None
            for ct in range(16):
                wt, wr = featmajor_tile(7760 + ct * 128)
                a, ar = acc[ct % 2]
                for tc in range(5):
                    t0 = F0 + tc * 512
                    n = min(512, TV - t0)
                    ps, psr = c.psum()
                    mm8(ps[:, 0:n], lambda kt: wt[:, kt, :], lambda kt, t0=t0, n=n: hT[:, kt, t0:t0 + n],
                        [wr] + hTres[t0 // 128:(t0 + n) // 128], [psr])
                    c.op("act", lambda e, a=a, ps=ps, tc=tc, n=n: e.activation(out=a[:, tc * 512:tc * 512 + n], in_=ps[:, 0:n],
                                                                               func=AF.Sigmoid), [psr], [ar])
                c.dma("sp", GMT[ct * 128:(ct + 1) * 128, :], a[:, 0:TF], [ar], [GMTr], accsem[ct % 2])
        c.barrier()
        sP12.close()

        c.barrier()
        chk(6)

        with ExitStack() as s3b:
            wcb = [c.sb([128, 2044], F32, "wcb", s3b) for _ in range(2)]
            wcs = [c.dsem("wcb") for _ in range(2)]
            wco = [c.dsem("wco") for _ in range(2)]
            for s in range(NS):
                wt_, wr_ = wcb[s % 2]
                src = cwin[s].rearrange("r c -> (r c)")[512:512 + 128 * 2044].rearrange("(p f) -> p f", p=128)
                dst = o_wins[s].rearrange("r c -> (r c)")[0:128 * 2044].rearrange("(p f) -> p f", p=128)
                c.dma("sp", wt_[:], src, [], [wr_], wcs[s % 2])
                c.dma("sp", dst, wt_[:], [wr_], [], wco[s % 2])
            for q in wco:
                ev = Res(); ev.w = {q: c.cnt[q]}; outs_res.append(ev)
            cvb, cvr = c.sb([NS, 3, 3072], F32, "cvb", s3b)
            cvs = c.dsem("cvb"); cvs2 = c.dsem("cvb2"); cvo = c.dsem("cvo")
            cvr2 = Res("cvr2")
            c.dma("sp", cvb[:, 0:2, :], st_conv[:, 1:3, :], [], [cvr], cvs)
            c.dma("sp", cvb[:, 2, :], PS[:, 2048:5120], [PSr], [cvr2], cvs2)
            c.dma("sp", o_convs, cvb[:], [cvr, cvr2], [], cvo)
            ev = Res(); ev.w = {cvo: c.cnt[cvo]}; outs_res.append(ev)
        c.barrier()

        sAttW = ExitStack()
        es.enter_context(sAttW)
        w1 = []; w2 = []; pehid, pehidr = c.sb([128, 2], F32, "pehid", sAttW)
        for s_, (w1d, w2d, ped) in enumerate(((w1k_d, w2k_d, pek_d), (w1v_d, w2v_d, pev_d)) if not (KX & 2) else ()):
            t, r = c.sb([64, 32, 128], BF16, "w1_%d" % s_, sAttW)
            c.dma("pool", t[:], w1d.rearrange("l d e -> d l e"), [], [r], c.dsem("w1"))
            w1.append((t, r))
            t2, r2 = c.sb([128, 64], BF16, "w2_%d" % s_, sAttW)
            c.dma("pool", t2[:], w2d, [], [r2], c.dsem("w2"))
            w2.append((t2, r2))
            pet, per = c.sb([64, 32], BF16, "peT%d" % s_, sAttW)
            vt, vr = vstg[next(vq)]
            c.dma("sp", vt[0:32, 0:64], ped, [], [vr], vsem[0] if vt is vstg[0][0] else vsem[1])
            ps, psr = c.psum()
            c.op("pe", lambda e, ps=ps, vt=vt: e.transpose(out=ps[0:64, 0:32], in_=vt[0:32, 0:64], identity=identf[0:32, 0:32]),
                 [vr, identfr], [psr])
            c.op("dve", lambda e, ps=ps, pet=pet: e.tensor_copy(out=pet[:], in_=ps[0:64, 0:32]), [psr], [per])
            ps, psr = c.psum()

            def f(e, ps=ps, t=t, pet=pet):
                for l in range(32):
                    ins = e.matmul(ps[:, 0:1], lhsT=t[:, l, :], rhs=pet[:, l:l + 1], start=(l == 0), stop=(l == 31))
                return ins
            c.op("pe", f, [r, per], [psr])
            c.op("dve", lambda e, ps=ps, s_=s_: e.tensor_copy(out=pehid[:, s_:s_ + 1], in_=ps[:, 0:1]), [psr], [pehidr])

        sAtt = ExitStack()
        es.enter_context(sAtt)
        KS, _ = c.sb([128, 4, TV], BF16, "KS", sAtt)
        KW, _ = c.sb([64, 4, TV], BF16, "KW", sAtt)
        VS, VSall = c.sb([128, NT, 4, 65], BF16, "VS", sAtt)
        VW, VWall = c.sb([128, NT, 4, 65], BF16, "VW", sAtt)
        KC, KCall = c.sb([64, 4, 256], BF16, "KC", sAtt)
        VCT, VCTall = c.sb([64, 4, 256], BF16, "VCT", sAtt)
        VCA, VCAr = c.sb([128, 2, 4, 129], BF16, "VCA", sAtt)
        KSres = [Res("KS%d" % i) for i in range(NT)]
        KWres = [Res("KW%d" % i) for i in range(NT)]
        VSres = [Res("VS%d" % i) for i in range(NT)]
        VWres = [Res("VW%d" % i) for i in range(NT)]
        KCres = [Res("KC%d" % i) for i in range(NT)]
        VCTres = [Res("VCT%d" % i) for i in range(NT)]
        KSind = Res("KSind")
        c.op("pool", lambda e: e.memset(VS[:, :, :, 64:65], 1.0), [], [VSall])
        c.op("pool", lambda e: e.memset(VW[:, :, :, 64:65], 1.0), [], [VWall])
        c.op("pool", lambda e: e.memset(KC[:], 0.0), [], [KCall])
        c.op("pool", lambda e: e.memset(VCT[:], 0.0), [], [VCTall])
        c.op("pool", lambda e: e.memset(VCA[:, :, :, 128:129], 1.0), [], [VCAr])
        for h in range(4):
            c.op("pool", lambda e, h=h: e.memset(KS[:, h, :], 1.0), [], [KSind])
            for c0 in (range(0, TV, 512) if not (KX & 1) else []):
                c.op("pool", lambda e, h=h, c0=c0: e.affine_select(
                    out=KS[:, h, c0:c0 + 512], in_=KS[:, h, c0:c0 + 512], pattern=[[1, 512]], compare_op=ALU.is_ge,
                    fill=0.0, base=4096 + c0, channel_multiplier=-64), [KSind], [KSind])
                c.op("pool", lambda e, h=h, c0=c0: e.affine_select(
                    out=KS[:, h, c0:c0 + 512], in_=KS[:, h, c0:c0 + 512], pattern=[[-1, 512]], compare_op=ALU.is_ge,
                    fill=0.0, base=63 - 4096 - c0, channel_multiplier=64), [KSind], [KSind])

        with ExitStack() as s3:
            kvb = [c.sb([128, 1536], F32, "kvb", s3) for _ in range(3)]
            kvsem = [c.dsem("kvb") for _ in range(3)]
            kvosem = [c.dsem("kvo") for _ in range(3)]
            csb_ = [c.sb([128, 16], F32, "cs", s3) for _ in range(3)]
            cssem = [c.dsem("cs") for _ in range(3)]
            tmp = [c.sb([128, 4, 3, 4, 8], F32, "ropetmp", s3) for _ in range(2)]
            kvbf = [c.sb([128, 1536], BF16, "kvbf", s3) for _ in range(2)]
            cmpb = [c.sb([64, 8, 144], BF16, "cmpb", s3) for _ in range(2)]
            cmpbh = [Res("cmpbh0"), Res("cmpbh1")]
            for q_ in range(2):
                c.op("pool", lambda e, q_=q_: e.memset(cmpb[q_][0][:, :, 0:16], 0.0), [], [cmpbh[q_]])
            hsb = [c.sb([128, 32], BF16, "hsb", s3) for _ in range(2)]

            def rope_tile(kvt, kvr, cs, csr_, rows, nstream, tm, tmr):
                v = kvt[0:rows, 0:nstream * 512].rearrange("p (s x h d) -> p s x h d", x=2, h=4, d=64)
                x1 = v[:, :, 0, :, 0:8]
                x2 = v[:, :, 0, :, 8:16]
                shp = [rows, nstream, 4, 8]
                cosb = cs[0:rows, 0:8].unsqueeze(1).unsqueeze(1).to_broadcast(shp)
                sinb = cs[0:rows, 8:16].unsqueeze(1).unsqueeze(1).to_broadcast(shp)
                t = [tm[0:rows, q, 0:nstream] for q in range(4)]
                c.op("dve", lambda e: e.tensor_tensor(out=t[0], in0=x1, in1=cosb, op=ALU.mult), [kvr, csr_], [tmr])
                c.op("dve", lambda e: e.tensor_tensor(out=t[1], in0=x2, in1=sinb, op=ALU.mult), [kvr, csr_], [tmr])
                c.op("dve", lambda e: e.tensor_tensor(out=t[2], in0=x2, in1=cosb, op=ALU.mult), [kvr, csr_], [tmr])
                c.op("dve", lambda e: e.tensor_tensor(out=t[3], in0=x1, in1=sinb, op=ALU.mult), [kvr, csr_], [tmr])
                c.op("dve", lambda e: e.tensor_tensor(out=x1, in0=t[0], in1=t[1], op=ALU.subtract), [tmr], [kvr])
                c.op("dve", lambda e: e.tensor_tensor(out=x2, in0=t[2], in1=t[3], op=ALU.add), [tmr], [kvr])

            for i in range(NT + 1):
                rows = 128 if i < NT else NS
                kvt, kvr = kvb[i % 3]
                cs, csr_ = csb_[i % 3]
                if i < NT:
                    c.dma("sp", kvt[:], TA[i * 128:(i + 1) * 128, A_KV:A_KV + 1536], [TAr], [kvr], kvsem[i % 3])
                    c.dma("sp", cs[:, :], csv[i * 128:(i + 1) * 128, :], [], [csr_], cssem[i % 3])
                else:
                    c.dma("sp", kvt[0:NS, :], PS[:, 6176:6176 + 1536], [PSr], [kvr], kvsem[i % 3])
                    c.dma("sp", cs[0:NS, :], css, [], [csr_], cssem[i % 3])
                tm, tmr = tmp[i % 2]
                rope_tile(kvt, kvr, cs, csr_, rows, 3, tm, tmr)

                if i < NT and not (KX & 4):
                    kb, kbr = kvbf[i % 2]
                    c.op("act", lambda e, kb=kb, kvt=kvt: e.copy(out=kb[:], in_=kvt[:]), [kvr], [kbr])
                    cs_ = slice(i * 128, (i + 1) * 128)
                    cb, cbr2 = cmpb[i % 2]
                    for bank, streams in (((0, (0, 1)), (1, (2, 4))) if not (KX & 32) else ()):
                        pt, pr = c.psum()
                        ptb = pt[:].bitcast(BF16)

                        def f(e, ptb=ptb, streams=streams, kb=kb):
                            for a_, st_ in enumerate(streams):
                                for h in range(4):
                                    col = st_ * 256 + h * 64
                                    ins = e.transpose(out=ptb[0:64, (a_ * 4 + h) * 128:(a_ * 4 + h + 1) * 128],
                                                      in_=kb[:, col:col + 64], identity=ident[:])
                            return ins
                        c.op("pe", f, [kbr, identr], [pr])
                        pv = ptb[0:64, :].rearrange("p (a n) -> p a n", n=128)
                        if bank == 0:
                            c.op("dve", lambda e, pv=pv, cb=cb: e.tensor_copy(out=cb[:, :, 16:144], in_=pv), [pr], [cbr2])
                        else:
                            c.op("dve", lambda e, pv=pv, cs_=cs_: e.tensor_copy(out=KS[0:64, :, cs_], in_=pv[:, 0:4, :]), [pr, KSind], [KSres[i]])
                            c.op("dve", lambda e, pv=pv, cs_=cs_: e.tensor_copy(out=KW[:, :, cs_], in_=pv[:, 4:8, :]), [pr], [KWres[i]])
                    if not (KX & 64):
                      c.op("pool", lambda e, kb=kb, i=i: e.tensor_copy(out=VS[:, i, :, 0:64],
                                                                    in_=kb[:, 768:1024].rearrange("p (h d) -> p h d", d=64)),
                         [kbr, VSall], [VSres[i]])
                      c.op("pool", lambda e, kb=kb, i=i: e.tensor_copy(out=VW[:, i, :, 0:64],
                                                                    in_=kb[:, 1280:1536].rearrange("p (h d) -> p h d", d=64)),
                         [kbr, VWall], [VWres[i]])
                    for s_ in (range(2) if not (KX & 10) else []):
                        w1t, w1r = w1[s_]; w2t, w2r = w2[s_]
                        ps, psr = c.psum()

                        def f(e, ps=ps, w1t=w1t, cb=cb, s_=s_):
                            for l in range(32):
                                ins = e.matmul(ps[:, 0:32], lhsT=w1t[:, l, :], rhs=cb[:, s_ * 4:(s_ + 1) * 4, l:l + 113:16],
                                               start=(l == 0), stop=(l == 31))
                            return ins
                        c.op("pe", f, [w1r, cbr2, cmpbh[i % 2]], [psr])
                        hs_, hsr = hsb[s_]
                        c.op("act", lambda e, ps=ps, hs_=hs_, s_=s_: e.activation(out=hs_[:], in_=ps[:, 0:32], func=AF.Silu,
                                                                                 bias=pehid[:, s_:s_ + 1]), [psr, pehidr], [hsr])
                        ps2, ps2r = c.psum()
                        c.op("pe", lambda e, ps2=ps2, w2t=w2t, hs_=hs_: e.matmul(ps2[0:64, 0:32], lhsT=w2t[:], rhs=hs_[:],
                                                                                start=True, stop=True), [w2r, hsr], [ps2r])
                        dst, dres, dall = (KC, KCres[i], KCall) if s_ == 0 else (VCT, VCTres[i], VCTall)
                        src = ps2[0:64, 0:32].rearrange("p (h j) -> p h j", j=8)
                        if i == 0:
                            c.op("dve", lambda e, dst=dst, src=src: e.tensor_copy(out=dst[:, :, 0:7], in_=src[:, :, 1:8]),
                                 [ps2r, dall], [dres])
                        else:
                            c.op("dve", lambda e, dst=dst, src=src, i=i: e.tensor_copy(out=dst[:, :, 8 * i - 1:8 * i + 7], in_=src),
                                 [ps2r, dall], [dres])
                    nb, nbr = cmpb[(i + 1) % 2]
                    c.op("pool", lambda e, nb=nb, cb=cb: e.tensor_copy(out=nb[:, :, 0:16], in_=cb[:, :, 128:144]),
                         [cbr2], [cmpbh[(i + 1) % 2]])
                osem = kvosem[i % 3]
                if i >= 16 and i < NT:
                    c.dma("sp", o_kvp[(i - 16) * 128:(i - 15) * 128, :], kvt[:, 0:1024], [kvr], [], osem)
                    if i >= 28:
                        c.dma("sp", o_winp[(i - 28) * 128:(i - 27) * 128, :], kvt[:, 1024:1536], [kvr], [], osem)
                elif i == NT:
                    c.dma("sp", KVS, kvt[0:NS, :], [kvr], [KVSr], osem)
                    c.dma("sp", o_kvs, kvt[0:NS, 0:1024], [kvr], [], osem)
                    c.dma("sp", o_wins[:, 511, :], kvt[0:NS, 1024:1536], [kvr], [], osem)
                ev = Res(); ev.w = {osem: c.cnt[osem]}; outs_res.append(ev)


        if not (KX & 16):
            dbg("KC", KC[:], [64, 4, 256], KCres + [KCall])
        if not (KX & 16):
            dbg("VCT", VCT[:], [64, 4, 256], VCTres + [VCTall])
        if not (KX & 16):
            dbg("KS0", KS[:, 0, 0:512], [128, 512], KSres[0:4] + [KSind])
        if not (KX & 16):
            dbg("KW1", KW[:, 1, 1792:2048], [64, 256], KWres[14:16])
        if not (KX & 16):
            dbg("VS14", VS[:, 14], [128, 4, 65], [VSres[14], VSall])
        c.barrier()
        chk(10)
        with ExitStack() as s4:
            ovt, ovr = c.sb([128, 2, 64], F32, "ovt", s4)
            c.dma("sp", ovt[:], ov_d.rearrange("(t n) j -> n t j", n=128), [], [ovr], c.dsem("ov"))
            pt, pr = c.psum()
            ptb = pt[:].bitcast(BF16)

            def f(e):
                for nt in range(2):
                    for h in range(4):
                        ins = e.transpose(out=ptb[:, (nt * 4 + h) * 64:(nt * 4 + h + 1) * 64], in_=VCT[:, h, nt * 128:(nt + 1) * 128],
                                          identity=ident[0:64, 0:64])
                return ins
            c.op("pe", f, VCTres + [VCTall, identr], [pr])
            c.op("dve", lambda e: e.tensor_copy(out=VCA[:, :, :, 0:64], in_=ptb[:, 0:512].rearrange("p (t h d) -> p t h d", t=2, h=4)),
                 [pr], [VCAr])
            c.op("dve", lambda e: e.tensor_copy(out=VCA[:, :, :, 64:128], in_=ovt[:].unsqueeze(2).to_broadcast([128, 2, 4, 64])),
                 [ovr, VCAr], [VCAr])
            c.barrier()

        with ExitStack() as s5:
            c.rot = [0, 1, 2, 3]
            trib, tribr = c.sb([128, 128], BF16, "trib", s5)
            gtb, gtbr = c.sb([128, 128], BF16, "gtb", s5)
            for t_, r_, pat, cm, op_ in ((trib, tribr, [[1, 128]], -1, ALU.is_ge), (gtb, gtbr, [[-1, 128]], 1, ALU.is_gt)):
                c.op("pool", lambda e, t_=t_: e.memset(t_[:], 1.0), [], [r_])
                c.op("pool", lambda e, t_=t_, pat=pat, cm=cm, op_=op_: e.affine_select(
                    out=t_[:], in_=t_[:], pattern=pat, compare_op=op_, fill=0.0, base=0, channel_multiplier=cm), [r_], [r_])
            biasc, biascr = c.sb([128, 2], F32, "biasc", s5)
            c.op("dve", lambda e: e.memset(biasc[:], 0.0), [], [biascr])
            c.op("dve", lambda e: e.tensor_scalar(out=biasc[:, 0:1], in0=pf[:, 0:1], scalar1=-1.0, scalar2=30000.0,
                                                  op0=ALU.add, op1=ALU.mult), [pfr, biascr], [biascr])
            qf = [c.sb([128, 1024], F32, "qf", s5) for _ in range(2)]
            qfsem = [c.dsem("qf") for _ in range(2)]
            qcs = [c.sb([128, 16], F32, "qcs", s5) for _ in range(2)]
            qcssem = [c.dsem("qcs") for _ in range(2)]
            gtl = [c.sb([128, 48], F32, "gtl", s5) for _ in range(2)]
            gtsem = [c.dsem("gtl") for _ in range(2)]
            mcb = [c.sb([128, 2, 128], BF16, "mcb", s5) for _ in range(2)]
            mcsem = [c.dsem("mcb") for _ in range(2)]
            scb = [c.sb([128, 3, 64], F32, "scb", s5) for _ in range(2)]
            scsem = [c.dsem("scb") for _ in range(2)]
            qtmp, qtmpr = c.sb([128, 4, 16, 8], F32, "qtmp", s5)
            qb, qbr = c.sb([128, 1024], BF16, "qb", s5)
            RQ, RQr = c.sb([128, 4, 4, 128], BF16, "RQ", s5)
            RQm = [Res("RQm%d" % h) for h in range(4)]
            pcb = [c.sb([128, 512], BF16, "pcb", s5) for _ in range(2)]
            ptile = [c.sb([128, 512], BF16, "ptile", s5) for _ in range(3)]
            pq_ = rr(3)
            sm_, smr = c.sb([128, 64], F32, "sm", s5)
            imp, impr = c.sb([128, 64], F32, "imp", s5)
            impw, impwr = c.sb([128, 64], F32, "impw", s5)
            mx8, mx8r = c.sb([128, 16], F32, "mx8", s5)
            nmt, nmtr = c.sb([128, 128], F32, "nmt", s5)
            c.op("pool", lambda e: e.memset(nmt[:], 0.0), [], [nmtr])
            yatt, yattr = c.sb([128, 1024], F32, "yatt", s5)
            yab, yabr = c.sb([128, 1024], BF16, "yab", s5)
            yaT = [c.sb([128, 8, 128], BF16, "yaT", s5) for _ in range(2)]
            yasem = [c.dsem("yaT") for _ in range(2)]
            sig, sigr = c.sb([128, 16, 3], F32, "sig", s5)
            ps_s, ps_sr = c.psum_fixed(4)
            ps_w, ps_wr = c.psum_fixed(5)
            ps_c = [c.psum_fixed(6), c.psum_fixed(7)]

            for i in range(F0T, NT):
                ii = i - F0T
                b2 = ii % 2
                qt, qr = qf[b2]; cs, csr_ = qcs[b2]; gt_, gtr_ = gtl[b2]; mc, mcr = mcb[b2]; scn, scnr = scb[b2]
                rows_ = slice(i * 128, (i + 1) * 128)
                c.dma("sp", qt[:], TA[rows_, A_Q:A_Q + 1024], [TAr], [qr], qfsem[b2])
                c.dma("sp", cs[:], csv[rows_, :], [], [csr_], qcssem[b2])
                c.dma("sp", gt_[:], TA[rows_, A_G:A_G + 48], [TAr], [gtr_], gtsem[b2])
                c.dma("pool", mc[:], maskC_d[ii].rearrange("t n q -> n t q"), [], [mcr], mcsem[b2])
                c.dma("sp", scn[:], selc_d[ii], [], [scnr], scsem[b2])
                v = qt[:].rearrange("p (h d) -> p h d", d=64)
                x1 = v[:, :, 0:8]; x2 = v[:, :, 8:16]
                shp = [128, 16, 8]
                cosb = cs[:, 0:8].unsqueeze(1).to_broadcast(shp); sinb = cs[:, 8:16].unsqueeze(1).to_broadcast(shp)
                t = [qtmp[:, q] for q in range(4)]
                c.op("dve", lambda e: e.tensor_tensor(out=t[0], in0=x1, in1=cosb, op=ALU.mult), [qr, csr_], [qtmpr])
                c.op("dve", lambda e: e.tensor_tensor(out=t[1], in0=x2, in1=sinb, op=ALU.mult), [qr, csr_], [qtmpr])
                c.op("dve", lambda e: e.tensor_tensor(out=t[2], in0=x2, in1=cosb, op=ALU.mult), [qr, csr_], [qtmpr])
                c.op("dve", lambda e: e.tensor_tensor(out=t[3], in0=x1, in1=sinb, op=ALU.mult), [qr, csr_], [qtmpr])
                c.op("dve", lambda e: e.tensor_tensor(out=x1, in0=t[0], in1=t[1], op=ALU.subtract), [qtmpr], [qr])
                c.op("dve", lambda e: e.tensor_tensor(out=x2, in0=t[2], in1=t[3], op=ALU.add), [qtmpr], [qr])
                c.op("act", lambda e: e.activation(out=qb[:], in_=qt[:], func=AF.Copy, scale=SCALE), [qr], [qbr])
                c.op("act", lambda e: e.activation(out=sig[:].rearrange("p a b -> p (a b)"), in_=gt_[:], func=AF.Sigmoid), [gtr_], [sigr])
                for bk in range(2):
                    pt, pr = c.psum()
                    ptb = pt[:].bitcast(BF16)

                    def f(e, ptb=ptb, bk=bk):
                        for q in range(8):
                            hd = bk * 8 + q
                            ins = e.transpose(out=ptb[0:64, q * 128:(q + 1) * 128], in_=qb[:, hd * 64:(hd + 1) * 64], identity=ident[:])
                        return ins
                    c.op("pe", f, [qbr, identr], [pr])
                    c.op("dve", lambda e, ptb=ptb, bk=bk: e.tensor_copy(
                        out=RQ[0:64, 2 * bk:2 * bk + 2, :, :].rearrange("p a g t -> p (a g) t"),
                        in_=ptb[0:64, :].rearrange("p (q t) -> p q t", t=128)), [pr], [RQr])

                for h in range(4):
                    rq64 = RQ[0:64, h].rearrange("p g t -> p (g t)")
                    rq128 = RQ[:, h].rearrange("p g t -> p (g t)")
                    for nt in range(2):
                        ps, psr = c.psum()
                        c.op("pe", lambda e, ps=ps, nt=nt: e.matmul(ps[:], lhsT=KC[:, h, nt * 128:(nt + 1) * 128], rhs=rq64,
                                                                    start=True, stop=True), KCres[0:i + 1] + [KCall, RQr], [psr])
                        pc, pcr = pcb[nt]
                        c.op("act", lambda e, ps=ps, pc=pc, nt=nt: e.activation(out=pc[:], in_=ps[:], func=AF.Exp, bias=biasc[:, nt:nt + 1]),
                             [psr, biascr], [pcr])
                        c.op("dve", lambda e, pc=pc, nt=nt: e.tensor_tensor(
                            out=pc[:].rearrange("p (g t) -> p g t", t=128), in0=pc[:].rearrange("p (g t) -> p g t", t=128),
                            in1=mc[:, nt, :].unsqueeze(1).to_broadcast([128, 4, 128]), op=ALU.mult), [pcr, mcr], [pcr])
                    for g in range(4):
                        pcx, pcxr = ps_c[g // 2]
                        tgt = pcx[:, (g % 2) * 129:(g % 2) * 129 + 129]

                        def f(e, tgt=tgt, g=g):
                            for nt in range(2):
                                ins = e.matmul(tgt, lhsT=pcb[nt][0][:, g * 128:(g + 1) * 128], rhs=VCA[:, nt, h, :],
                                               start=(nt == 0), stop=(nt == 1))
                            return ins
                        c.op("pe", f, [pcb[0][1], pcb[1][1], VCAr], [pcxr])
                    for a_ in range(2):
                        c.op("dve", lambda e, a_=a_: e.tensor_copy(
                            out=sm_[:, 2 * a_:2 * a_ + 2], in_=ps_c[a_][0][:, 0:258].rearrange("p (g c) -> p g c", c=129)[:, :, 128]),
                            [ps_c[a_][1]], [smr])
                    c.op("dve", lambda e: e.tensor_scalar(out=sm_[:, 4:8], in0=sm_[:, 0:4], scalar1=1e-30, scalar2=None, op0=ALU.max),
                         [smr], [smr])
                    c.op("dve", lambda e: e.reciprocal(out=sm_[:, 4:8], in_=sm_[:, 4:8]), [smr], [smr])
                    for g in range(4):
                        src = ps_c[g // 2][0][:, (g % 2) * 129 + 64:(g % 2) * 129 + 128]
                        if g == 0:
                            c.op("dve", lambda e, src=src: e.tensor_scalar(out=imp[:], in0=src, scalar1=sm_[:, 4:5], scalar2=None,
                                                                          op0=ALU.mult), [ps_c[0][1], smr], [impr])
                        else:
                            c.op("dve", lambda e, src=src, g=g: e.scalar_tensor_tensor(
                                out=imp[:], in0=src, scalar=sm_[:, 4 + g:5 + g], in1=imp[:], op0=ALU.mult, op1=ALU.add),
                                [ps_c[g // 2][1], smr, impr], [impr])
                    c.op("dve", lambda e: e.tensor_tensor(out=imp[:], in0=imp[:], in1=scn[:, 0, :], op=ALU.mult), [impr, scnr], [impr])
                    c.op("dve", lambda e: e.tensor_tensor(out=imp[:], in0=imp[:], in1=scn[:, 1, :], op=ALU.add), [impr, scnr], [impr])
                    c.op("dve", lambda e: e.max(out=mx8[:, 0:8], in_=imp[:]), [impr], [mx8r])
                    c.op("dve", lambda e: e.match_replace(out=impw[:], in_to_replace=mx8[:, 0:8], in_values=imp[:], imm_value=-1e9),
                         [impr, mx8r], [impwr])
                    c.op("dve", lambda e: e.max(out=mx8[:, 8:16], in_=impw[:]), [impwr], [mx8r])
                    c.op("dve", lambda e: e.tensor_scalar(out=impw[:], in0=imp[:], scalar1=mx8[:, 15:16], scalar2=None, op0=ALU.is_ge),
                         [impr, mx8r], [impwr])
                    c.op("dve", lambda e: e.tensor_tensor(out=impw[:], in0=impw[:], in1=scn[:, 2, :], op=ALU.mult), [impwr, scnr], [impwr])
                    c.op("dve", lambda e: e.tensor_scalar(out=nmt[:, 64:128], in0=impw[:], scalar1=30000.0, scalar2=-30000.0,
                                                          op0=ALU.mult, op1=ALU.add), [impwr], [nmtr])
                    ps, psr = c.psum()
                    c.op("pe", lambda e, ps=ps: e.transpose(out=ps[:, 0:128], in_=nmt[:], identity=identf[:]), [nmtr, identfr], [psr])
                    c.op("dve", lambda e, ps=ps: e.tensor_copy(out=RQ[64:128, h],
                                                               in_=ps[64:128, 0:128].unsqueeze(1).to_broadcast([64, 4, 128])),
                         [psr, RQr], [RQm[h]])
                    LA = 2
                    qk_ = {}

                    def issue_qk_s(kt):
                        ps, psr = c.psum()
                        c.op("pe", lambda e, ps=ps, kt=kt: e.matmul(ps[:], lhsT=KS[:, h, kt * 128:(kt + 1) * 128], rhs=rq128,
                                                                    start=True, stop=True), [KSres[kt], KSind, RQr, RQm[h]], [psr])
                        qk_[kt] = (ps, psr)
                    for kt in range(min(LA, i + 1)):
                        issue_qk_s(kt)
                    for kt in range(i + 1):
                        if kt + LA <= i:
                            issue_qk_s(kt + LA)
                        ps, psr = qk_.pop(kt)
                        p_, p_r = ptile[next(pq_)]
                        c.op("act", lambda e, ps=ps, p_=p_: e.activation(out=p_[:], in_=ps[:], func=AF.Exp), [psr], [p_r])
                        if kt == i:
                            c.op("dve", lambda e, p_=p_: e.tensor_tensor(
                                out=p_[:].rearrange("p (g t) -> p g t", t=128), in0=p_[:].rearrange("p (g t) -> p g t", t=128),
                                in1=trib[:].unsqueeze(1).to_broadcast([128, 4, 128]), op=ALU.mult), [p_r, tribr], [p_r])

                        def f(e, p_=p_, kt=kt):
                            for g in range(4):
                                ins = e.matmul(ps_s[:, g * 65:(g + 1) * 65], lhsT=p_[:, g * 128:(g + 1) * 128], rhs=VS[:, kt, h, :],
                                               start=(kt == 0 and g == 0), stop=(kt == i), skip_group_check=True)
                            return ins
                        c.op("pe", f, [p_r, VSres[kt], VSall], [ps_sr])
                    qkw_ = {}

                    def issue_qk_w(kt):
                        ps, psr = c.psum()
                        c.op("pe", lambda e, ps=ps, kt=kt: e.matmul(ps[:], lhsT=KW[:, h, kt * 128:(kt + 1) * 128], rhs=rq64,
                                                                    start=True, stop=True), [KWres[kt], RQr], [psr])
                        qkw_[kt] = (ps, psr)
                    for kt in range(i - 4, i - 4 + LA):
                        issue_qk_w(kt)
                    for kt in range(i - 4, i + 1):
                        if kt + LA <= i:
                            issue_qk_w(kt + LA)
                        ps, psr = qkw_.pop(kt)
                        p_, p_r = ptile[next(pq_)]
                        bcol = biasc[:, 0:1] if kt < 16 else biasc[:, 1:2]
                        c.op("act", lambda e, ps=ps, p_=p_, bcol=bcol: e.activation(out=p_[:], in_=ps[:], func=AF.Exp, bias=bcol),
                             [psr, biascr], [p_r])
                        if kt == i or kt == i - 4:
                            mk, mkr = (trib, tribr) if kt == i else (gtb, gtbr)
                            c.op("dve", lambda e, p_=p_, mk=mk: e.tensor_tensor(
                                out=p_[:].rearrange("p (g t) -> p g t", t=128), in0=p_[:].rearrange("p (g t) -> p g t", t=128),
                                in1=mk[:].unsqueeze(1).to_broadcast([128, 4, 128]), op=ALU.mult), [p_r, mkr], [p_r])

                        def f(e, p_=p_, kt=kt):
                            for g in range(4):
                                ins = e.matmul(ps_w[:, g * 65:(g + 1) * 65], lhsT=p_[:, g * 128:(g + 1) * 128], rhs=VW[:, kt, h, :],
                                               start=(kt == i - 4 and g == 0), stop=(kt == i), skip_group_check=True)
                            return ins
                        c.op("pe", f, [p_r, VWres[kt], VWall], [ps_wr])
                    for o_, (pz, pzr) in ((8, (ps_s, ps_sr)), (16, (ps_w, ps_wr))):
                        c.op("dve", lambda e, o_=o_, pz=pz: e.tensor_scalar(
                            out=sm_[:, o_ + 4:o_ + 8], in0=pz[:, 0:260].rearrange("p (g c) -> p g c", c=65)[:, :, 64],
                            scalar1=1e-30, scalar2=None, op0=ALU.max), [pzr], [smr])
                        c.op("dve", lambda e, o_=o_: e.reciprocal(out=sm_[:, o_ + 4:o_ + 8], in_=sm_[:, o_ + 4:o_ + 8]), [smr], [smr])
                    for br, o_ in enumerate((4, 12, 20)):
                        c.op("dve", lambda e, br=br, o_=o_: e.tensor_tensor(out=sm_[:, 24 + 4 * br:28 + 4 * br], in0=sm_[:, o_:o_ + 4],
                                                                            in1=sig[:, 4 * h:4 * h + 4, br], op=ALU.mult), [smr, sigr], [smr])
                    for g in range(4):
                        dst = yatt[:, (4 * h + g) * 64:(4 * h + g + 1) * 64]
                        srcc = ps_c[g // 2][0][:, (g % 2) * 129:(g % 2) * 129 + 64]
                        c.op("dve", lambda e, dst=dst, srcc=srcc, g=g: e.tensor_scalar(out=dst, in0=srcc, scalar1=sm_[:, 24 + g:25 + g],
                                                                                      scalar2=None, op0=ALU.mult),
                             [ps_c[g // 2][1], smr], [yattr])
                        c.op("dve", lambda e, dst=dst, g=g: e.scalar_tensor_tensor(
                            out=dst, in0=ps_s[:, g * 65:g * 65 + 64], scalar=sm_[:, 28 + g:29 + g], in1=dst, op0=ALU.mult, op1=ALU.add),
                            [ps_sr, smr, yattr], [yattr])
                        c.op("dve", lambda e, dst=dst, g=g: e.scalar_tensor_tensor(
                            out=dst, in0=ps_w[:, g * 65:g * 65 + 64], scalar=sm_[:, 32 + g:33 + g], in1=dst, op0=ALU.mult, op1=ALU.add),
                            [ps_wr, smr, yattr], [yattr])
                c.op("act", lambda e: e.copy(out=yab[:], in_=yatt[:]), [yattr], [yabr])
                pt, pr = c.psum()
                ptb = pt[:].bitcast(BF16)

                def f(e, ptb=ptb):
                    for q in range(8):
                        ins = e.transpose(out=ptb[:, q * 128:(q + 1) * 128], in_=yab[:, q * 128:(q + 1) * 128], identity=ident[:])
                    return ins
                c.op("pe", f, [yabr, identr], [pr])
                ya, yar = yaT[b2]
                c.op("dve", lambda e, ptb=ptb, ya=ya: e.tensor_copy(out=ya[:].rearrange("p a t -> p (a t)"), in_=ptb), [pr], [yar])
                c.dma("sp", YAT.rearrange("(a p) n -> p a n", p=128)[:, :, ii * 128:(ii + 1) * 128], ya[:], [yar], [YATr], yasem[b2])
                if ii == 0:
                    dbg("yatt14", yatt[:], [128, 1024], [yattr])
            c.rot = list(range(8))

        sAtt.close()
        with ExitStack() as sa:
            c.rot = [0, 1, 2, 3]
            NK = 17 * 128
            KSs, KSsr = c.sb([128, 4, NK], BF16, "KSs", sa)
            VSs, VSsr = c.sb([128, 17, 4, 128], BF16, "VSs", sa)
            KWs, KWsr = c.sb([64, 4, 5 * 128], BF16, "KWs", sa)
            VWs, VWsr = c.sb([128, 5, 4, 128], BF16, "VWs", sa)
            KCs, KCsr = c.sb([64, 4, 128], BF16, "KCs", sa)
            VCTs, VCTsr = c.sb([64, 4, 128], BF16, "VCTs", sa)
            VC2, VC2r = c.sb([128, 4, 128], BF16, "VC2", sa)
            c.op("pool", lambda e: e.memset(VSs[:, 16], 0.0), [], [VSsr])
            c.op("pool", lambda e: e.memset(VWs[:, 4], 0.0), [], [VWsr])
            c.op("pool", lambda e: e.memset(KWs[:, :, 512:640], 0.0), [], [KWsr])
            c.op("pool", lambda e: e.memset(KCs[:], 0.0), [], [KCsr])
            c.op("pool", lambda e: e.memset(VCTs[:], 0.0), [], [VCTsr])
            for h in range(4):
                c.op("pool", lambda e, h=h: e.memset(KSs[:, h, :], 1.0), [], [KSsr])
                for c0 in range(0, NK, 128):
                    c.op("pool", lambda e, h=h, c0=c0: e.affine_select(
                        out=KSs[:, h, c0:c0 + 128], in_=KSs[:, h, c0:c0 + 128], pattern=[[1, 128]], compare_op=ALU.is_ge,
                        fill=0.0, base=4096 + c0, channel_multiplier=-64), [KSsr], [KSsr])
                    c.op("pool", lambda e, h=h, c0=c0: e.affine_select(
                        out=KSs[:, h, c0:c0 + 128], in_=KSs[:, h, c0:c0 + 128], pattern=[[-1, 128]], compare_op=ALU.is_ge,
                        fill=0.0, base=63 - 4096 - c0, channel_multiplier=64), [KSsr], [KSsr])
                c.op("pool", lambda e, h=h: e.memset(KSs[0:64, h, 2048:NK], 0.0), [KSsr], [KSsr])
            smk, smkr = c.sb([128, 3], F32, "smk", sa)
            c.dma("sp", smk[:], smask_d, [], [smkr], c.dsem("smk"))
            scn, scnr = c.sb([1, 3, 64], F32, "scn", sa)
            c.dma("sp", scn[:].rearrange("p a b -> p (a b)"), sconst_d, [], [scnr], c.dsem("scn"))
            ov1, ov1r = c.sb([128, 65], BF16, "ov1", sa)
            ov1f, ov1fr = c.sb([128, 65], F32, "ov1f", sa)
            c.dma("sp", ov1f[:], ovs_d, [], [ov1fr], c.dsem("ov1"))
            c.op("dve", lambda e: e.tensor_copy(out=ov1[:], in_=ov1f[:]), [ov1fr], [ov1r])
            onec, onecr = c.sb([128, 1], BF16, "onec", sa)
            c.op("pool", lambda e: e.memset(onec[:], 1.0), [], [onecr])
            oner, onerr = c.sb([1, 128], F32, "oner", sa)
            c.op("pool", lambda e: e.memset(oner[:], 1.0), [], [onerr])
            pti, ptir = c.sb([128, NS * 16], I32, "pti", sa)
            c.dma("sp", pti[:], ptab_d.partition_broadcast(128), [], [ptir], c.dsem("pti"))
            iop, iopr = c.sb([128, 1], F32, "iop", sa)
            c.dma("sp", iop[:], iotap_d, [], [iopr], c.dsem("iop"))
            ptf, ptfr = c.sb([128, NS * 16], F32, "ptf", sa)
            c.op("dve", lambda e: e.tensor_copy(out=ptf[:], in_=pti[:]), [ptir], [ptfr])
            c.op("dve", lambda e: e.tensor_scalar(out=ptf[:], in0=ptf[:], scalar1=128.0, scalar2=iop[:, 0:1], op0=ALU.mult, op1=ALU.add),
                 [ptfr, iopr], [ptfr])
            idx, idxr = c.sb([128, NS * 16], I32, "idx", sa)
            c.op("dve", lambda e: e.tensor_copy(out=idx[:], in_=ptf[:]), [ptfr], [idxr])
            gs, gsr = c.sb([NS, 48], F32, "gs", sa)
            c.dma("sp", gs[:], PS[:, 7712:7760], [PSr], [gsr], c.dsem("gs"))
            c.op("act", lambda e: e.activation(out=gs[:], in_=gs[:], func=AF.Sigmoid), [gsr], [gsr])
            c.dma("sp", SG, gs[:], [gsr], [SGr], c.dsem("gso"))
            qs_, qsr = c.sb([NS, 1024], F32, "qs", sa)
            c.dma("sp", qs_[:], PS[:, 5152:6176], [PSr], [qsr], c.dsem("qs"))
            cst, cstr = c.sb([NS, 16], F32, "cst", sa)
            c.dma("sp", cst[:], css, [], [cstr], c.dsem("cst"))
            qtm, qtmr = c.sb([NS, 4, 16, 8], F32, "qtm", sa)
            v = qs_[:].rearrange("p (h d) -> p h d", d=64)
            x1 = v[:, :, 0:8]; x2 = v[:, :, 8:16]
            shp = [NS, 16, 8]
            cosb = cst[:, 0:8].unsqueeze(1).to_broadcast(shp); sinb = cst[:, 8:16].unsqueeze(1).to_broadcast(shp)
            t = [qtm[:, q] for q in range(4)]
            c.op("dve", lambda e: e.tensor_tensor(out=t[0], in0=x1, in1=cosb, op=ALU.mult), [qsr, cstr], [qtmr])
            c.op("dve", lambda e: e.tensor_tensor(out=t[1], in0=x2, in1=sinb, op=ALU.mult), [qsr, cstr], [qtmr])
            c.op("dve", lambda e: e.tensor_tensor(out=t[2], in0=x2, in1=cosb, op=ALU.mult), [qsr, cstr], [qtmr])
            c.op("dve", lambda e: e.tensor_tensor(out=t[3], in0=x1, in1=sinb, op=ALU.mult), [qsr, cstr], [qtmr])
            c.op("dve", lambda e: e.tensor_tensor(out=x1, in0=t[0], in1=t[1], op=ALU.subtract), [qtmr], [qsr])
            c.op("dve", lambda e: e.tensor_tensor(out=x2, in0=t[2], in1=t[3], op=ALU.add), [qtmr], [qsr])
            qsb, qsbr = c.sb([NS, 1024], BF16, "qsb", sa)
            c.op("act", lambda e: e.activation(out=qsb[:], in_=qs_[:], func=AF.Copy, scale=SCALE), [qsr], [qsbr])
            QT, QTr = c.sb([64, 16, NS], BF16, "QT", sa)
            pt, pr = c.psum()
            ptb = pt[:].bitcast(BF16)

            def f(e):
                for hd in range(16):
                    ins = e.transpose(out=ptb[0:64, hd * NS:(hd + 1) * NS], in_=qsb[:, hd * 64:(hd + 1) * 64], identity=ident[0:NS, 0:NS])
                return ins
            c.op("pe", f, [qsbr, identr], [pr])
            c.op("dve", lambda e: e.tensor_copy(out=QT[:].rearrange("p a s -> p (a s)"), in_=ptb[0:64, 0:16 * NS]), [pr], [QTr])
            kvn, kvnr = c.sb([NS, 1536], F32, "kvn", sa)
            c.dma("sp", kvn[:], KVS, [KVSr], [kvnr], c.dsem("kvn"))
            kvnb, kvnbr = c.sb([NS, 1536], BF16, "kvnb", sa)
            c.op("act", lambda e: e.copy(out=kvnb[:], in_=kvn[:]), [kvnr], [kvnbr])
            KNT, KNTr = c.sb([64, 2, 4, NS], BF16, "KNT", sa)
            pt, pr = c.psum()
            ptb = pt[:].bitcast(BF16)

            def f(e):
                for a_, st_ in enumerate((2, 4)):
                    for h in range(4):
                        col = st_ * 256 + h * 64
                        ins = e.transpose(out=ptb[0:64, (a_ * 4 + h) * NS:(a_ * 4 + h + 1) * NS], in_=kvnb[:, col:col + 64],
                                          identity=ident[0:NS, 0:NS])
                return ins
            c.op("pe", f, [kvnbr, identr], [pr])
            c.op("dve", lambda e: e.tensor_copy(out=KNT[:].rearrange("p a h s -> p (a h s)"), in_=ptb[0:64, 0:8 * NS]), [pr], [KNTr])

            pg = [c.sb([128, 1024], F32, "pg", sa) for _ in range(3)]
            pgsem = [c.dsem("pg") for _ in range(3)]
            pgb = [c.sb([128, 1024], BF16, "pgb", sa) for _ in range(2)]
            cmps = [c.sb([64, 8, 144], BF16, "cmps", sa) for _ in range(2)]
            cmpsh = [Res("cmpsh0"), Res("cmpsh1")]
            hss = [c.sb([128, 32], BF16, "hss", sa) for _ in range(2)]
            wnf, wnfr = c.sb([128, 4, 512], F32, "wnf", sa)
            wnsem = c.dsem("wnf")
            wnb, wnbr = c.sb([128, 4, 512], BF16, "wnb", sa)
            vnew, vnewr = c.sb([1, 2, 256], BF16, "vnew", sa)
            vnsem = c.dsem("vnew")
            RQs, RQsr = c.sb([128, 16], BF16, "RQs", sa)
            RQm, RQmr = Res("RQsm"), None
            pcs, pcsr = c.sb([128, 16], BF16, "pcs", sa)
            pts = [c.sb([128, 16], BF16, "pts", sa) for _ in range(3)]
            ptq = rr(3)
            t4, t4r = c.sb([4, 4, 65], F32, "t4", sa)
            rz4, rz4r = c.sb([4, 4], F32, "rz4", sa)
            imp4, imp4r = c.sb([1, 4, 64], F32, "imp4", sa)
            wk4, wk4r = c.sb([1, 4, 64], F32, "wk4", sa)
            mxs, mxsr = c.sb([1, 4, 16], F32, "mxs", sa)
            ngm, ngmr = c.sb([1, 4, 128], F32, "ngm", sa)
            c.op("pool", lambda e: e.memset(ngm[:], 0.0), [], [ngmr])
            zr, zrr = c.sb([1, 3, 16], F32, "zr", sa)
            gat, gatr = c.sb([1, 48], F32, "gat", sa)
            gatsem = c.dsem("gat")
            coef, coefr = c.sb([1, 3, 16], F32, "coef", sa)
            bcs, bcsr_ = c.sb([128, 3, 16], F32, "bcs", sa)
            yo, yor = c.sb([128, 16], F32, "yo", sa)
            yo2, yo2r = c.sb([128, 16], F32, "yo2", sa)
            ps_os, ps_osr = c.psum_fixed(4)
            ps_ow, ps_owr = c.psum_fixed(5)
            ps_oc, ps_ocr = c.psum_fixed(6)
            k_ = 0
            for s in range(NS):
                c.op("pool", lambda e: e.memset(cmps[0][0][:, :, 0:16], 0.0), [cmpsh[0]], [cmpsh[0]])
                for j in range(16):
                    pgt, pgr = pg[k_ % 3]
                    c._wait("pool", [idxr], [pgr])
                    c.cnt[pgsem[k_ % 3]] += 16
                    nc.gpsimd.indirect_dma_start(
                        out=pgt[:, :], out_offset=None, in_=cache_d[:, :],
                        in_offset=bass.IndirectOffsetOnAxis(ap=idx[:, s * 16 + j:s * 16 + j + 1], axis=0),
                    ).then_inc(c.sem[pgsem[k_ % 3]], 16)
                    c._mark((pgsem[k_ % 3], c.cnt[pgsem[k_ % 3]]), [idxr], [pgr])
                    k_ += 1
                    pb, pbr = pgb[j % 2]
                    c.op("act", lambda e, pb=pb, pgt=pgt: e.copy(out=pb[:], in_=pgt[:]), [pgr], [pbr])
                    cb, cbr2 = cmps[j % 2]
                    cs_ = slice(j * 128, (j + 1) * 128)
                    for bank in range(2):
                        pt, pr = c.psum()
                        ptb = pt[:].bitcast(BF16)
                        nst = 2 if bank == 0 else 1

                        def f(e, ptb=ptb, bank=bank, nst=nst, pb=pb):
                            for a_ in range(nst):
                                st_ = a_ if bank == 0 else 2
                                for h in range(4):
                                    col = st_ * 256 + h * 64
                                    ins = e.transpose(out=ptb[0:64, (a_ * 4 + h) * 128:(a_ * 4 + h + 1) * 128],
                                                      in_=pb[:, col:col + 64], identity=ident[:])
                            return ins
                        c.op("pe", f, [pbr, identr], [pr])
                        pv = ptb[0:64, :].rearrange("p (a n) -> p a n", n=128)
                        if bank == 0:
                            c.op("dve", lambda e, pv=pv, cb=cb: e.tensor_copy(out=cb[:, :, 16:144], in_=pv), [pr], [cbr2])
                        else:
                            c.op("dve", lambda e, pv=pv, cs_=cs_: e.tensor_copy(out=KSs[0:64, :, cs_], in_=pv[:, 0:4, :]), [pr, KSsr], [KSsr])
                    for dup in range(2):
                        c.op("pool", lambda e, pb=pb, j=j, dup=dup: e.tensor_copy(
                            out=VSs[:, j, :, dup * 64:(dup + 1) * 64], in_=pb[:, 768:1024].rearrange("p (h d) -> p h d", d=64)),
                            [pbr], [VSsr])
                    for s_ in range(2):
                        w1t, w1r = w1[s_]; w2t, w2r = w2[s_]
                        ps, psr = c.psum()

                        def f(e, ps=ps, w1t=w1t, cb=cb, s_=s_):
                            for l in range(32):
                                ins = e.matmul(ps[:, 0:32], lhsT=w1t[:, l, :], rhs=cb[:, s_ * 4:(s_ + 1) * 4, l:l + 113:16],
                                               start=(l == 0), stop=(l == 31))
                            return ins
                        c.op("pe", f, [w1r, cbr2, cmpsh[j % 2]], [psr])
                        hs_, hsr = hss[s_]
                        c.op("act", lambda e, ps=ps, hs_=hs_, s_=s_: e.activation(out=hs_[:], in_=ps[:, 0:32], func=AF.Silu,
                                                                                 bias=pehid[:, s_:s_ + 1]), [psr, pehidr], [hsr])
                        ps2, ps2r = c.psum()
                        c.op("pe", lambda e, ps2=ps2, w2t=w2t, hs_=hs_: e.matmul(ps2[0:64, 0:32], lhsT=w2t[:], rhs=hs_[:],
                                                                                start=True, stop=True), [w2r, hsr], [ps2r])
                        dst, dres = (KCs, KCsr) if s_ == 0 else (VCTs, VCTsr)
                        src = ps2[0:64, 0:32].rearrange("p (h j) -> p h j", j=8)
                        if j == 0:
                            c.op("dve", lambda e, dst=dst, src=src: e.tensor_copy(out=dst[:, :, 0:7], in_=src[:, :, 1:8]), [ps2r, dres], [dres])
                        else:
                            c.op("dve", lambda e, dst=dst, src=src, j=j: e.tensor_copy(out=dst[:, :, 8 * j - 1:8 * j + 7], in_=src),
                                 [ps2r, dres], [dres])
                    nb, nbr = cmps[(j + 1) % 2]
                    c.op("pool", lambda e, nb=nb, cb=cb: e.tensor_copy(out=nb[:, :, 0:16], in_=cb[:, :, 128:144]), [cbr2], [cmpsh[(j + 1) % 2]])
                c.op("dve", lambda e, s=s: e.tensor_copy(out=KSs[0:64, :, 2048], in_=KNT[:, 0, :, s]), [KNTr, KSsr], [KSsr])
                c.op("dve", lambda e, s=s: e.tensor_copy(out=KWs[:, :, 512], in_=KNT[:, 1, :, s]), [KNTr, KWsr], [KWsr])
                c._wait("pool", [KVSr], [vnewr]);
                c.dma("pool", vnew[:, 0, :], KVS[s:s + 1, 768:1024], [KVSr], [vnewr], vnsem)
                vn2 = Res("vn2")
                c.dma("pool", vnew[:, 1, :], KVS[s:s + 1, 1280:1536], [KVSr], [vn2], vnsem)
                for dup in range(2):
                    c.op("pool", lambda e, dup=dup: e.tensor_copy(out=VSs[0:1, 16, :, dup * 64:(dup + 1) * 64],
                                                                  in_=vnew[:, 0, :].rearrange("p (h d) -> p h d", d=64)), [vnewr, vn2, VSsr], [VSsr])
                    c.op("pool", lambda e, dup=dup: e.tensor_copy(out=VWs[0:1, 4, :, dup * 64:(dup + 1) * 64],
                                                                  in_=vnew[:, 1, :].rearrange("p (h d) -> p h d", d=64)), [vnewr, vn2, VWsr], [VWsr])
                c.dma("sp", wnf[:], cwin[s].rearrange("(t p) c -> p t c", p=128), [], [wnfr], wnsem)
                c.op("act", lambda e: e.copy(out=wnb[:], in_=wnf[:]), [wnfr], [wnbr])
                for half_ in range(2):
                    pt, pr = c.psum()
                    ptb = pt[:].bitcast(BF16)

                    def f(e, ptb=ptb, half_=half_):
                        for a_ in range(2):
                            wt_ = half_ * 2 + a_
                            for h in range(4):
                                ins = e.transpose(out=ptb[0:64, (a_ * 4 + h) * 128:(a_ * 4 + h + 1) * 128],
                                                  in_=wnb[:, wt_, h * 64:(h + 1) * 64], identity=ident[:])
                        return ins
                    c.op("pe", f, [wnbr, identr], [pr])
                    for a_ in range(2):
                        wt_ = half_ * 2 + a_
                        c.op("dve", lambda e, ptb=ptb, a_=a_, wt_=wt_: e.tensor_copy(
                            out=KWs[:, :, wt_ * 128:(wt_ + 1) * 128], in_=ptb[0:64, a_ * 512:(a_ + 1) * 512].rearrange("p (h n) -> p h n", n=128)),
                            [pr, KWsr], [KWsr])
                for dup in range(2):
                    c.op("pool", lambda e, dup=dup: e.tensor_copy(out=VWs[:, 0:4, :, dup * 64:(dup + 1) * 64],
                                                                  in_=wnb[:, :, 256:512].rearrange("p t (h d) -> p t h d", d=64)),
                         [wnbr, VWsr], [VWsr])
                c.op("dve", lambda e, s=s: e.tensor_copy(out=RQs[0:64, :], in_=QT[:, :, s]), [QTr, RQsr], [RQsr])
                pt, pr = c.psum()
                ptb = pt[:].bitcast(BF16)

                def f(e, ptb=ptb):
                    for h in range(4):
                        ins = e.transpose(out=ptb[:, h * 64:(h + 1) * 64], in_=VCTs[:, h, :], identity=ident[0:64, 0:64])
                    return ins
                c.op("pe", f, [VCTsr, identr], [pr])
                for dup in range(2):
                    c.op("dve", lambda e, ptb=ptb, dup=dup: e.tensor_copy(out=VC2[:, :, dup * 64:(dup + 1) * 64],
                                                                          in_=ptb[:, 0:256].rearrange("p (h d) -> p h d", d=64)), [pr, VC2r], [VC2r])
                ps, psr = c.psum()

                def f(e, ps=ps):
                    for h in range(4):
                        ins = e.matmul(ps[:, 4 * h:4 * h + 4], lhsT=KCs[:, h, :], rhs=RQs[0:64, 4 * h:4 * h + 4], start=True, stop=True)
                    return ins
                c.op("pe", f, [KCsr, RQsr], [psr])
                c.op("act", lambda e, ps=ps: e.activation(out=pcs[:], in_=ps[:, 0:16], func=AF.Exp), [psr], [pcsr])
                c.op("dve", lambda e: e.tensor_scalar(out=pcs[:], in0=pcs[:], scalar1=smk[:, 2:3], scalar2=None, op0=ALU.mult), [pcsr, smkr], [pcsr])
                ps, psr = c.psum()

                def f(e, ps=ps):
                    for h in range(4):
                        ins = e.matmul(ps[0:4, h * 65:(h + 1) * 65], lhsT=pcs[:, 4 * h:4 * h + 4], rhs=ov1[:], start=True, stop=True)
                    return ins
                c.op("pe", f, [pcsr, ov1r], [psr])
                c.op("dve", lambda e, ps=ps: e.tensor_copy(out=t4[:].rearrange("p a b -> p (a b)"), in_=ps[0:4, 0:260]), [psr], [t4r])
                c.op("dve", lambda e: e.tensor_scalar(out=rz4[:], in0=t4[:, :, 64], scalar1=1e-30, scalar2=None, op0=ALU.max), [t4r], [rz4r])
                c.op("dve", lambda e: e.reciprocal(out=rz4[:], in_=rz4[:]), [rz4r], [rz4r])
                ps, psr = c.psum()

                def f(e, ps=ps):
                    for h in range(4):
                        ins = e.matmul(ps[0:1, h * 64:(h + 1) * 64], lhsT=rz4[:, h:h + 1], rhs=t4[:, h, 0:64], start=True, stop=True)
                    return ins
                c.op("pe", f, [rz4r, t4r], [psr])
                c.op("dve", lambda e, ps=ps: e.tensor_tensor(out=imp4[:], in0=ps[0:1, 0:256].rearrange("p (h j) -> p h j", j=64),
                                                            in1=scn[:, 0:1, :].to_broadcast([1, 4, 64]), op=ALU.mult), [psr, scnr], [imp4r])
                c.op("dve", lambda e: e.tensor_tensor(out=imp4[:], in0=imp4[:], in1=scn[:, 1:2, :].to_broadcast([1, 4, 64]), op=ALU.add),
                     [imp4r, scnr], [imp4r])

                def f(e, ps_oc=ps_oc):
                    for h in range(4):
                        e.matmul(ps_oc[:, 4 * h:4 * h + 4], lhsT=VC2[:, h, :], rhs=pcs[:, 4 * h:4 * h + 4], start=(h == 0), stop=True,
                                 skip_group_check=True)
                    return e.matmul(ps_oc[0:1, 32:48], lhsT=onec[:], rhs=pcs[:], start=False, stop=True, skip_group_check=True)
                c.op("pe", f, [VC2r, pcsr, onecr], [ps_ocr])
                for h in range(4):
                    c.op("dve", lambda e, h=h: e.max(out=mxs[:, h, 0:8], in_=imp4[:, h, :]), [imp4r], [mxsr])
                    c.op("dve", lambda e, h=h: e.match_replace(out=wk4[:, h, :], in_to_replace=mxs[:, h, 0:8], in_values=imp4[:, h, :],
                                                              imm_value=-1e9), [imp4r, mxsr], [wk4r])
                    c.op("dve", lambda e, h=h: e.max(out=mxs[:, h, 8:16], in_=wk4[:, h, :]), [wk4r], [mxsr])
                    c.op("dve", lambda e, h=h: e.tensor_scalar(out=wk4[:, h, :], in0=imp4[:, h, :], scalar1=mxs[:, h, 15:16], scalar2=None,
                                                              op0=ALU.is_ge), [imp4r, mxsr], [wk4r])
                c.op("dve", lambda e: e.tensor_tensor(out=wk4[:], in0=wk4[:], in1=scn[:, 2:3, :].to_broadcast([1, 4, 64]), op=ALU.mult),
                     [wk4r, scnr], [wk4r])
                c.op("dve", lambda e: e.tensor_scalar(out=ngm[:, :, 64:128], in0=wk4[:], scalar1=30000.0, scalar2=-30000.0,
                                                      op0=ALU.mult, op1=ALU.add), [wk4r], [ngmr])
                ps, psr = c.psum()

                def f(e, ps=ps):
                    for h in range(4):
                        ins = e.transpose(out=ps[:, h:h + 1], in_=ngm[:, h, :], identity=identf[0:1, 0:1])
                    return ins
                c.op("pe", f, [ngmr, identfr], [psr])
                c.op("dve", lambda e, ps=ps: e.tensor_copy(out=RQs[64:128, :].rearrange("p (h g) -> p h g", g=4),
                                                           in_=ps[64:128, 0:4].unsqueeze(2).to_broadcast([64, 4, 4])), [psr, RQsr], [RQsr])
                for kt in range(17):
                    ps, psr = c.psum()

                    def f(e, ps=ps, kt=kt):
                        for h in range(4):
                            ins = e.matmul(ps[:, 4 * h:4 * h + 4], lhsT=KSs[:, h, kt * 128:(kt + 1) * 128], rhs=RQs[:, 4 * h:4 * h + 4],
                                           start=True, stop=True)
                        return ins
                    c.op("pe", f, [KSsr, RQsr], [psr])
                    p_, p_r = pts[next(ptq)]
                    c.op("act", lambda e, ps=ps, p_=p_: e.activation(out=p_[:], in_=ps[:, 0:16], func=AF.Exp), [psr], [p_r])
                    if kt == 16:
                        c.op("dve", lambda e, p_=p_: e.tensor_scalar(out=p_[:], in0=p_[:], scalar1=smk[:, 0:1], scalar2=None, op0=ALU.mult),
                             [p_r, smkr], [p_r])

                    def f(e, p_=p_, kt=kt):
                        for h in range(4):
                            e.matmul(ps_os[:, 4 * h:4 * h + 4], lhsT=VSs[:, kt, h, :], rhs=p_[:, 4 * h:4 * h + 4],
                                     start=(kt == 0 and h == 0), stop=(kt == 16), skip_group_check=True)
                        return e.matmul(ps_os[0:1, 32:48], lhsT=onec[:], rhs=p_[:], start=False, stop=(kt == 16), skip_group_check=True)
                    c.op("pe", f, [p_r, VSsr, onecr], [ps_osr])
                for kt in range(5):
                    ps, psr = c.psum()

                    def f(e, ps=ps, kt=kt):
                        for h in range(4):
                            ins = e.matmul(ps[:, 4 * h:4 * h + 4], lhsT=KWs[:, h, kt * 128:(kt + 1) * 128], rhs=RQs[0:64, 4 * h:4 * h + 4],
                                           start=True, stop=True)
                        return ins
                    c.op("pe", f, [KWsr, RQsr], [psr])
                    p_, p_r = pts[next(ptq)]
                    c.op("act", lambda e, ps=ps, p_=p_: e.activation(out=p_[:], in_=ps[:, 0:16], func=AF.Exp), [psr], [p_r])
                    if kt == 0 or kt == 4:
                        mcol = smk[:, 1:2] if kt == 0 else smk[:, 0:1]
                        c.op("dve", lambda e, p_=p_, mcol=mcol: e.tensor_scalar(out=p_[:], in0=p_[:], scalar1=mcol, scalar2=None, op0=ALU.mult),
                             [p_r, smkr], [p_r])

                    def f(e, p_=p_, kt=kt):
                        for h in range(4):
                            e.matmul(ps_ow[:, 4 * h:4 * h + 4], lhsT=VWs[:, kt, h, :], rhs=p_[:, 4 * h:4 * h + 4],
                                     start=(kt == 0 and h == 0), stop=(kt == 4), skip_group_check=True)
                        return e.matmul(ps_ow[0:1, 32:48], lhsT=onec[:], rhs=p_[:], start=False, stop=(kt == 4), skip_group_check=True)
                    c.op("pe", f, [p_r, VWsr, onecr], [ps_owr])
                for br, (pz, pzr) in enumerate(((ps_oc, ps_ocr), (ps_os, ps_osr), (ps_ow, ps_owr))):
                    c.op("dve", lambda e, br=br, pz=pz: e.tensor_scalar(out=zr[:, br, :], in0=pz[0:1, 32:48], scalar1=1e-30, scalar2=None,
                                                                       op0=ALU.max), [pzr], [zrr])
                c.op("dve", lambda e: e.reciprocal(out=zr[:], in_=zr[:]), [zrr], [zrr])
                c.dma("sp", gat[:], SG[s:s + 1, :], [SGr], [gatr], gatsem)
                c.op("dve", lambda e: e.tensor_tensor(out=coef[:], in0=zr[:], in1=gat[:].rearrange("p (hd b) -> p b hd", b=3), op=ALU.mult),
                     [zrr, gatr], [coefr])
                ps, psr = c.psum()
                c.op("pe", lambda e, ps=ps: e.matmul(ps[:, 0:48], lhsT=oner[:], rhs=coef[:].rearrange("p a b -> p (a b)"), start=True, stop=True),
                     [onerr, coefr], [psr])
                c.op("dve", lambda e, ps=ps: e.tensor_copy(out=bcs[:].rearrange("p a b -> p (a b)"), in_=ps[:, 0:48]), [psr], [bcsr_])
                c.op("dve", lambda e: e.tensor_tensor(out=yo[:], in0=ps_oc[:, 0:16], in1=bcs[:, 0, :], op=ALU.mult), [ps_ocr, bcsr_], [yor])
                c.op("dve", lambda e: e.tensor_tensor(out=yo2[:], in0=ps_os[:, 0:16], in1=bcs[:, 1, :], op=ALU.mult), [ps_osr, bcsr_], [yo2r])
                c.op("dve", lambda e: e.tensor_tensor(out=yo[:], in0=yo[:], in1=yo2[:], op=ALU.add), [yor, yo2r], [yor])
                c.op("dve", lambda e: e.tensor_tensor(out=yo2[:], in0=ps_ow[:, 0:16], in1=bcs[:, 2, :], op=ALU.mult), [ps_owr, bcsr_], [yo2r])
                c.op("dve", lambda e: e.tensor_tensor(out=yo[:], in0=yo[:], in1=yo2[:], op=ALU.add), [yor, yo2r], [yor])
                yv = yo[:].rearrange("p (a two) -> p a two", two=2)
                c.op("dve", lambda e, s=s: e.tensor_copy(out=yaTs_p[0:64, :, s], in_=yv[0:64, :, 0]), [yor, yaTs_pr], [yaTs_pr])
                c.op("dve", lambda e, s=s: e.tensor_copy(out=yaTs_p[64:128, :, s], in_=yv[64:128, :, 1]), [yor, yaTs_pr], [yaTs_pr])
            c.rot = list(range(8))
        c.barrier()
        sAttW.close()
        chk(11)

        GN_ENV = {}

        def gate_norm_T(rows, y_ap, yr, z_ap, zr, dst_fn, dst_r):
            ybf, ybfr, gst, gstr, snw, snwr = (GN_ENV[q] for q in ("ybf", "ybfr", "gst", "gstr", "snw", "snwr"))
            c.op("act", lambda e: e.activation(out=z_ap, in_=z_ap, func=AF.Silu), [zr], [zr])
            c.op("dve", lambda e: e.tensor_tensor(out=y_ap, in0=y_ap, in1=z_ap, op=ALU.mult), [yr, zr], [yr])
            c.op("act", lambda e: e.activation(out=z_ap, in_=y_ap, func=AF.Square, accum_out=gst[0:rows, 0:1]), [yr], [zr, gstr])
            c.op("act", lambda e: e.activation(out=gst[0:rows, 1:2], in_=gst[0:rows, 0:1], func=AF.Sqrt, scale=1.0 / 2048, bias=EPS),
                 [gstr], [gstr])
            c.op("dve", lambda e: e.reciprocal(out=gst[0:rows, 2:3], in_=gst[0:rows, 1:2]), [gstr], [gstr])
            c.op("dve", lambda e: e.tensor_scalar(out=ybf[0:rows, :], in0=y_ap, scalar1=gst[0:rows, 2:3], scalar2=None, op0=ALU.mult),
                 [yr, gstr], [ybfr])
            for hlf in range(2):
                pt, pr = c.psum()
                ptb = pt[:].bitcast(BF16)

                def f(e, ptb=ptb, hlf=hlf):
                    for q in range(8):
                        ct = hlf * 8 + q
                        ins = e.transpose(out=ptb[:, q * 128:q * 128 + rows], in_=ybf[0:rows, ct * 128:(ct + 1) * 128],
                                          identity=ident[0:rows, 0:rows])
                    return ins
                c.op("pe", f, [ybfr, identr], [pr])
                c.op("dve", lambda e, ptb=ptb, hlf=hlf: e.tensor_tensor(
                    out=dst_fn(hlf), in0=ptb.rearrange("p (q c) -> p q c", c=128)[:, :, 0:rows],
                    in1=snw[:, hlf * 8:(hlf + 1) * 8].unsqueeze(2).to_broadcast([128, 8, rows]), op=ALU.mult),
                    [pr, snwr], [dst_r])


        c.barrier()
        chk(7)
        ynTs, ynTsr = c.sb([128, 16, NS], BF16, "ynTs")
        with ExitStack() as s6:
            def cmat(name, pattern, cm, op):
                t, r = c.sb([128, 128], F32, name, s6)
                c.op("pool", lambda e: e.memset(t[:], 1.0), [], [r])
                c.op("pool", lambda e: e.affine_select(out=t[:], in_=t[:], pattern=pattern, compare_op=op, fill=0.0,
                                                       base=0, channel_multiplier=cm), [r], [r])
                return t, r
            triu, triur = cmat("triu", [[1, 128]], -1, ALU.is_ge)
            ustr, ustrr = cmat("ustr", [[-1, 128]], 1, ALU.is_gt)
            ones, onesr = c.sb([128, 128], F32, "ones", s6)
            c.op("pool", lambda e: e.memset(ones[:], 1.0), [], [onesr])
            par, parr = c.sb([128, 3, 32], F32, "par", s6)
            psem = c.dsem("par")
            parr.multi = True
            c.dma("sp", par[:, 0, :], dt_bias.partition_broadcast(128), [], [parr], psem)
            c.dma("sp", par[:, 1, :], A_log.partition_broadcast(128), [], [parr], psem)
            c.dma("sp", par[:, 2, :], ssm_Dv.partition_broadcast(128), [], [parr], psem)
            parr2 = Res("par2")
            c.op("act", lambda e: e.activation(out=par[:, 1, :], in_=par[:, 1, :], func=AF.Exp), [parr], [parr2])
            c.op("dve", lambda e: e.tensor_scalar(out=par[:, 1, :], in0=par[:, 1, :], scalar1=-1.0, scalar2=None, op0=ALU.mult),
                 [parr2], [parr2])
            snw, snwr = c.sb([128, 16], F32, "snw", s6)
            load_vecT(snw[:], ssm_nw.rearrange("(t p) -> t p", p=128), 16, snwr)

            H, Hr = c.sb([128, 4, 512], F32, "H", s6)
            Hb, Hbr = c.sb([128, 4, 512], BF16, "Hb", s6)
            c.op("pool", lambda e: e.memset(H[:], 0.0), [], [Hr])
            c.op("pool", lambda e: e.memset(Hb[:], 0.0), [], [Hbr])

            xcb = [c.sb([128, 24, 128], F32, "xcb", s6) for _ in range(2)]
            xcsem = [c.dsem("xcb") for _ in range(2)]
            dtb = [c.sb([128, 32], F32, "dtb", s6) for _ in range(2)]
            dtsem = [c.dsem("dtb") for _ in range(2)]
            zb = [c.sb([128, 2048], F32, "zb", s6) for _ in range(2)]
            zsem = [c.dsem("zb") for _ in range(2)]
            sc, scr = c.sb([128, 8, 32], F32, "sc", s6)
            xtm, xtmr = c.sb([128, 2048], F32, "xtm", s6)
            btm, btmr = c.sb([128, 4, 128], BF16, "btm", s6)
            bct, bctr = c.sb([128, 8, 128], BF16, "bct", s6)
            cbm, cbmr = c.sb([128, 4, 128], F32, "cbm", s6)
            vd = [c.sb([128, 4, 128], F32, "vd", s6) for _ in range(2)]
            lt = [c.sb([128, 4, 128], F32, "lt", s6) for _ in range(2)]
            mt = [c.sb([128, 4, 128], BF16, "mt", s6) for _ in range(2)]
            xdt, xdtr = c.sb([128, 32, 64], BF16, "xdt", s6)
            xdtd, xdtdr = c.sb([128, 32, 64], BF16, "xdtd", s6)
            xdtf, xdtfr = c.sb([128, 32, 64], F32, "xdtf", s6)
            yb, ybr = c.sb([128, 2048], F32, "yb", s6)
            ytmp, ytmpr = c.sb([128, 2048], F32, "ytmp", s6)
            ybf, ybfr = c.sb([128, 2048], BF16, "ybf", s6)
            ynt = [c.sb([128, 16, 128], BF16, "ynt", s6) for _ in range(2)]
            yntsem = [c.dsem("ynt") for _ in range(2)]
            gst, gstr = c.sb([128, 4], F32, "gst", s6)
            XCTv = XCT.rearrange("(t p) n -> p t n", p=128)
            c.rot = [0, 1, 2, 3]

            def softplus_dt(rows, dtt, dtr):
                c.op("dve", lambda e: e.tensor_tensor(out=sc[0:rows, 0, :], in0=dtt[0:rows, :], in1=par[0:rows, 0, :], op=ALU.add),
                     [dtr, parr], [scr])
                c.op("act", lambda e: e.activation(out=sc[0:rows, 0, :], in_=sc[0:rows, 0, :], func=AF.Exp), [scr], [scr])
                c.op("act", lambda e: e.activation(out=sc[0:rows, 0, :], in_=sc[0:rows, 0, :], func=AF.Ln, bias=1.0), [scr], [scr])
                c.op("dve", lambda e: e.tensor_tensor(out=sc[0:rows, 1, :], in0=sc[0:rows, 0, :], in1=par[0:rows, 1, :], op=ALU.mult),
                     [scr, parr2], [scr])

            GN_ENV.update(ybf=ybf, ybfr=ybfr, gst=gst, gstr=gstr, snw=snw, snwr=snwr)
            for i in range(NT):
                full = i >= F0T
                xc, xcr = xcb[i % 2]
                dtt, dtr = dtb[i % 2]
                c.dma("sp", xc[:], XCTv[:, :, i * 128:(i + 1) * 128], [XCTr], [xcr], xcsem[i % 2])
                c.dma("sp", dtt[:], TA[i * 128:(i + 1) * 128, A_DT:A_DT + 32], [TAr], [dtr], dtsem[i % 2])
                if full:
                    zt, zr = zb[i % 2]
                    c.dma("sp", zt[:], Zs[(i - F0T) * 128:(i - F0T + 1) * 128, :], [Zr], [zr], zsem[i % 2])
                if i == 16:
                    c.op("dve", lambda e: e.tensor_scalar(out=H[:], in0=H[:], scalar1=pf[:, 0:1], scalar2=None, op0=ALU.mult),
                         [Hr, pfr], [Hr])
                    c.op("act", lambda e: e.copy(out=Hb[:], in_=H[:]), [Hr], [Hbr])
                softplus_dt(128, dtt, dtr)
                ps, psr = c.psum()

                def f(e, ps=ps):
                    e.matmul(ps[:, 0:32], lhsT=triu[:], rhs=sc[:, 1, :], start=True, stop=True)
                    return e.matmul(ps[:, 32:64], lhsT=ones[:], rhs=sc[:, 1, :], start=True, stop=True)
                c.op("pe", f, [triur, onesr, scr], [psr])
                c.op("dve", lambda e, ps=ps: e.tensor_copy(out=sc[:, 2:4, :], in_=ps[:, 0:64].rearrange("p (a b) -> p a b", b=32)),
                     [psr], [scr])
                c.op("dve", lambda e: e.tensor_tensor(out=sc[:, 7, :], in0=sc[:, 3, :], in1=sc[:, 2, :], op=ALU.subtract), [scr], [scr])
                c.op("act", lambda e: e.activation(out=sc[:, 4, :], in_=sc[:, 7, :], func=AF.Exp), [scr], [scr])
                c.op("act", lambda e: e.activation(out=sc[:, 5:7, :], in_=sc[:, 2:4, :], func=AF.Exp), [scr], [scr])
                for q4 in range(5):
                    ps, psr = c.psum()

                    def f(e, ps=ps, q4=q4, xc=xc):
                        for q in range(4):
                            ins = e.transpose(out=ps[:, q * 128:(q + 1) * 128], in_=xc[:, q4 * 4 + q, :], identity=identf[:])
                        return ins
                    c.op("pe", f, [xcr, identfr], [psr])
                    if q4 < 4:
                        evac(xtm[:, q4 * 512:(q4 + 1) * 512], ps[:], [psr], [xtmr])
                    else:
                        evac(btm[:].rearrange("p g n -> p (g n)"), ps[:], [psr], [btmr])
                c.op("pool", lambda e, xc=xc: e.tensor_copy(out=bct[:], in_=xc[:, 16:24, :]), [xcr], [bctr])
                xv3 = xtm[:].rearrange("p (h d) -> p h d", d=64)
                c.op("dve", lambda e: e.tensor_tensor(out=xdtf[:], in0=xv3, in1=sc[:, 0, :].unsqueeze(2).to_broadcast([128, 32, 64]),
                                                      op=ALU.mult), [xtmr, scr], [xdtfr])
                c.op("pool", lambda e: e.tensor_tensor(out=xdtd[:], in0=xdtf[:], in1=sc[:, 4, :].unsqueeze(2).to_broadcast([128, 32, 64]),
                                                       op=ALU.mult), [xdtfr, scr], [xdtdr])
                if full:
                    c.op("act", lambda e: e.copy(out=xdt[:], in_=xdtf[:]), [xdtfr], [xdtr])
                    ps, psr = c.psum()

                    def f(e, ps=ps):
                        for g in range(4):
                            ins = e.matmul(ps[:, g * 128:(g + 1) * 128], lhsT=bct[:, g, :], rhs=bct[:, 4 + g, :], start=True, stop=True)
                        return ins
                    c.op("pe", f, [bctr], [psr])
                    c.op("dve", lambda e, ps=ps: e.tensor_tensor(
                        out=cbm[:], in0=ps[:].rearrange("p (g l) -> p g l", l=128),
                        in1=triu[:].unsqueeze(1).to_broadcast([128, 4, 128]), op=ALU.mult), [psr, triur], [cbmr])
                    ydiag = []
                    for g in range(4):
                        py, pyr = c.psum_fixed(4 + g)
                        ydiag.append((py, pyr))
                        for hq in range(2):
                            h0 = g * 8 + hq * 4
                            j = (g * 2 + hq) % 2
                            vdt, vdr = vd[j]; ltt, ltr = lt[j]; mtt, mtr = mt[j]
                            c.op("dve", lambda e, vdt=vdt, h0=h0: e.tensor_tensor(
                                out=vdt[:], in0=triu[:].unsqueeze(1).to_broadcast([128, 4, 128]),
                                in1=sc[:, 1, h0:h0 + 4].unsqueeze(2).to_broadcast([128, 4, 128]), op=ALU.mult),
                                [triur, scr], [vdr])
                            ps, psr = c.psum()

                            def f(e, ps=ps, vdt=vdt):
                                for q in range(4):
                                    ins = e.matmul(ps[:, q * 128:(q + 1) * 128], lhsT=ustr[:], rhs=vdt[:, q, :], start=True, stop=True)
                                return ins
                            c.op("pe", f, [ustrr, vdr], [psr])
                            c.op("act", lambda e, ps=ps, ltt=ltt: e.activation(out=ltt[:].rearrange("p a b -> p (a b)"), in_=ps[:],
                                                                               func=AF.Exp), [psr], [ltr])
                            c.op("pool", lambda e, ltt=ltt, mtt=mtt, g=g: e.tensor_tensor(
                                out=mtt[:], in0=ltt[:], in1=cbm[:, g, :].unsqueeze(1).to_broadcast([128, 4, 128]), op=ALU.mult),
                                [ltr, cbmr], [mtr])

                            def f(e, py=py, mtt=mtt, h0=h0, hq=hq):
                                for q in range(4):
                                    ins = e.matmul(py[:, (hq * 4 + q) * 64:(hq * 4 + q + 1) * 64], lhsT=mtt[:, q, :],
                                                   rhs=xdt[:, h0 + q, :], start=True, stop=True)
                                return ins
                            c.op("pe", f, [mtr, xdtr], [pyr])
                    for g in range(4):
                        ps, psr = c.psum()
                        c.op("pe", lambda e, ps=ps, g=g: e.matmul(ps[:], lhsT=bct[:, 4 + g, :], rhs=Hb[:, g, :], start=True, stop=True),
                             [bctr, Hbr], [psr])
                        sl = slice(g * 512, (g + 1) * 512)
                        c.op("dve", lambda e, ps=ps, g=g, sl=sl: e.tensor_tensor(
                            out=ytmp[:, sl].rearrange("p (h d) -> p h d", d=64), in0=ps[:].rearrange("p (h d) -> p h d", d=64),
                            in1=sc[:, 5, g * 8:(g + 1) * 8].unsqueeze(2).to_broadcast([128, 8, 64]), op=ALU.mult), [psr, scr], [ytmpr])
                        py, pyr = ydiag[g]
                        c.op("dve", lambda e, py=py, sl=sl: e.tensor_tensor(out=yb[:, sl], in0=py[:], in1=ytmp[:, sl], op=ALU.add),
                             [pyr, ytmpr], [ybr])
                    c.op("pool", lambda e: e.tensor_tensor(out=ytmp[:].rearrange("p (h d) -> p h d", d=64), in0=xv3,
                                                           in1=par[:, 2, :].unsqueeze(2).to_broadcast([128, 32, 64]), op=ALU.mult),
                         [xtmr, parr, ytmpr], [ytmpr])
                    c.op("pool", lambda e: e.tensor_tensor(out=yb[:], in0=yb[:], in1=ytmp[:], op=ALU.add), [ybr, ytmpr], [ybr])
                c.op("pool", lambda e: e.tensor_tensor(out=H[:].rearrange("p g (r d) -> p (g r) d", d=64),
                                                       in0=H[:].rearrange("p g (r d) -> p (g r) d", d=64),
                                                       in1=sc[:, 6, :].unsqueeze(2).to_broadcast([128, 32, 64]), op=ALU.mult),
                     [Hr, scr, Hbr], [Hr])
                for g in range(4):
                    ps, psr = c.psum()
                    c.op("pe", lambda e, ps=ps, g=g: e.matmul(ps[:], lhsT=btm[:, g, :],
                                                              rhs=xdtd[:, g * 8:(g + 1) * 8, :].rearrange("p h d -> p (h d)"),
                                                              start=True, stop=True), [btmr, xdtdr], [psr])
                    c.op("dve", lambda e, ps=ps, g=g: e.tensor_tensor(out=H[:, g, :], in0=H[:, g, :], in1=ps[:], op=ALU.add),
                         [Hr, psr], [Hr])
                c.op("act", lambda e: e.copy(out=Hb[:], in_=H[:]), [Hr], [Hbr])
                if i == 0:
                    dbg("sc0", sc[:], [128, 8, 32], [scr])
                    dbg("H0", H[:], [128, 4, 512], [Hr])
                    dbg("xtm0", xtm[:], [128, 2048], [xtmr])
                    dbg("xc0", xc[:], [128, 24, 128], [xcr])
                if i == F0T:
                    dbg("yb14", yb[:], [128, 2048], [ybr])
                    dbg("H14", H[:], [128, 4, 512], [Hr])
                if full:
                    yn_, ynr = ynt[i % 2]
                    gate_norm_T(128, yb[:], ybr, zt[:], zr, lambda hlf, yn_=yn_: yn_[:, hlf * 8:(hlf + 1) * 8, :], ynr)
                    c.dma("sp", YNT.rearrange("(t p) n -> p t n", p=128)[:, :, (i - F0T) * 128:(i - F0T + 1) * 128], yn_[:],
                          [ynr], [YNTr], yntsem[i % 2])

            hstg = [c.sb([128, 512], F32, "hstg", s6) for _ in range(2)]
            hsem = [c.dsem("hstg") for _ in range(2)]
            for g in range(4):
                ps, psr = c.psum()

                def f(e, ps=ps, g=g):
                    for q in range(4):
                        ins = e.transpose(out=ps[:, q * 128:(q + 1) * 128], in_=H[:, g, q * 128:(q + 1) * 128], identity=identf[:])
                    return ins
                c.op("pe", f, [Hr, identfr], [psr])
                hs_, hsr = hstg[g % 2]
                evac(hs_[:], ps[:], [psr], [hsr])
                c.dma("sp", o_ssmp[g * 512:(g + 1) * 512, :].rearrange("(q p) n -> p q n", p=128),
                      hs_[:].rearrange("p (q n) -> p q n", n=128), [hsr], [], hsem[g % 2])
            for q in hsem:
                ev = Res(); ev.w = {q: c.cnt[q]}; outs_res.append(ev)

            c.rot = list(range(8))
        c.barrier()
        chk(8)
        s6o = ExitStack()
        es.enter_context(s6o)
        xcs, xcsr = c.sb([NS, 3072], F32, "xcs", s6o)
        xcsr.multi = True
        with ExitStack() as s6s:
            cvx = [c.sb([NS, 4, 1024], F32, "cvx", s6s) for _ in range(2)]
            cvw = [c.sb([NS, 4, 1024], F32, "cvw", s6s) for _ in range(2)]
            cvbias = [c.sb([NS, 1024], F32, "cvbias", s6s) for _ in range(2)]
            cvsem = [[c.dsem("cv") for _ in range(4)] for _ in range(2)]
            for cc in range(3):
                j = cc % 2
                x4, x4r = cvx[j]; w4, w4r = cvw[j]; b4, b4r = cvbias[j]
                x4b = Res("x4b")
                cs_ = slice(cc * 1024, (cc + 1) * 1024)
                c.dma("sp", x4[:, 0:3, :], st_conv[:, :, cs_], [], [x4r], cvsem[j][0])
                c.dma("sp", x4[:, 3, :], PS[:, 2048 + cc * 1024:2048 + (cc + 1) * 1024], [PSr, x4r], [x4b], cvsem[j][1])
                c.dma("sp", w4[:], ssm_conv_w[:, cs_].partition_broadcast(NS), [], [w4r], cvsem[j][2])
                c.dma("sp", b4[:], ssm_conv_b[cs_].partition_broadcast(NS), [], [b4r], cvsem[j][3])
                c.op("dve", lambda e, x4=x4, w4=w4: e.tensor_tensor(out=x4[:], in0=x4[:], in1=w4[:], op=ALU.mult),
                     [x4r, x4b, w4r], [x4r])
                c.op("dve", lambda e, x4=x4: e.tensor_tensor(out=x4[:, 0:2, :], in0=x4[:, 0:2, :], in1=x4[:, 2:4, :], op=ALU.add),
                     [x4r], [x4r])
                c.op("dve", lambda e, x4=x4: e.tensor_tensor(out=x4[:, 0, :], in0=x4[:, 0, :], in1=x4[:, 1, :], op=ALU.add),
                     [x4r], [x4r])
                c.op("dve", lambda e, x4=x4, b4=b4: e.tensor_tensor(out=x4[:, 0, :], in0=x4[:, 0, :], in1=b4[:], op=ALU.add),
                     [x4r, b4r], [x4r])
                c.op("act", lambda e, x4=x4, cs_=cs_: e.activation(out=xcs[:, cs_], in_=x4[:, 0, :], func=AF.Silu), [x4r], [xcsr])
        c.barrier()
        with ExitStack() as s6t:
            sc, scr = c.sb([128, 8, 32], F32, "scs", s6t)
            par, parr = c.sb([128, 3, 32], F32, "pars", s6t)
            psem = c.dsem("pars")
            parr.multi = True
            c.dma("sp", par[:, 0, :], dt_bias.partition_broadcast(128), [], [parr], psem)
            c.dma("sp", par[:, 1, :], A_log.partition_broadcast(128), [], [parr], psem)
            c.dma("sp", par[:, 2, :], ssm_Dv.partition_broadcast(128), [], [parr], psem)
            parr2 = Res("par2s")
            c.op("act", lambda e: e.activation(out=par[:, 1, :], in_=par[:, 1, :], func=AF.Exp), [parr], [parr2])
            c.op("dve", lambda e: e.tensor_scalar(out=par[:, 1, :], in0=par[:, 1, :], scalar1=-1.0, scalar2=None, op0=ALU.mult),
                 [parr2], [parr2])
            dts, dtsr = c.sb([NS, 32], F32, "dts", s6t)
            c.dma("sp", dts[:], PS[:, 5120:5152], [PSr], [dtsr], c.dsem("dts"))
            c.op("dve", lambda e: e.tensor_tensor(out=sc[0:NS, 0, :], in0=dts[:], in1=par[0:NS, 0, :], op=ALU.add), [dtsr, parr], [scr])
            c.op("act", lambda e: e.activation(out=sc[0:NS, 0, :], in_=sc[0:NS, 0, :], func=AF.Exp), [scr], [scr])
            c.op("act", lambda e: e.activation(out=sc[0:NS, 0, :], in_=sc[0:NS, 0, :], func=AF.Ln, bias=1.0), [scr], [scr])
            c.op("dve", lambda e: e.tensor_tensor(out=sc[0:NS, 1, :], in0=sc[0:NS, 0, :], in1=par[0:NS, 1, :], op=ALU.mult),
                 [scr, parr2], [scr])
            c.op("act", lambda e: e.activation(out=sc[0:NS, 1, :], in_=sc[0:NS, 1, :], func=AF.Exp), [scr], [scr])
            rep, repr_ = c.sb([NS, 2, 4, 32], F32, "rep", s6t)
            c.op("dve", lambda e: e.tensor_copy(out=rep[:], in_=sc[0:NS, 0:2, :].unsqueeze(2).to_broadcast([NS, 2, 4, 32])),
                 [scr], [repr_])
            ps, psr = c.psum()

            def f(e):
                e.matmul(ps[:, 0:NS], lhsT=rep[:, 0].rearrange("s a b -> s (a b)"), rhs=identf[0:NS, 0:NS], start=True, stop=True)
                return e.matmul(ps[:, NS:2 * NS], lhsT=rep[:, 1].rearrange("s a b -> s (a b)"), rhs=identf[0:NS, 0:NS],
                                start=True, stop=True)
            c.op("pe", f, [repr_, identfr], [psr])
            dd, ddr = c.sb([128, 2, NS], F32, "dd", s6t)
            c.op("dve", lambda e: e.tensor_copy(out=dd[:].rearrange("p a s -> p (a s)"), in_=ps[:, 0:2 * NS]), [psr], [ddr])
            ind, indr = c.sb([128, 32], F32, "ind", s6t)
            c.op("dve", lambda e: e.tensor_tensor(out=ind[:], in0=identf[:, 0:32], in1=identf[:, 32:64], op=ALU.add), [identfr], [indr])
            c.op("dve", lambda e: e.tensor_tensor(out=ind[:], in0=ind[:], in1=identf[:, 64:96], op=ALU.add), [identfr, indr], [indr])
            c.op("dve", lambda e: e.tensor_tensor(out=ind[:], in0=ind[:], in1=identf[:, 96:128], op=ALU.add), [identfr, indr], [indr])
            c.op("dve", lambda e: e.tensor_tensor(out=ind[:], in0=ind[:], in1=par[:, 2, :], op=ALU.mult), [indr, parr], [indr])
            dcol, dcolr = c.sb([128, 1], F32, "dcol", s6t)
            c.op("dve", lambda e: e.reduce_sum(out=dcol[:], in_=ind[:], axis=AX.X), [indr], [dcolr])
            xperm, xpr = c.sb([NS, 16, 128], F32, "xperm", s6t)
            c.op("dve", lambda e: e.tensor_copy(
                out=xperm[:].rearrange("s pl (pq h) -> s pl pq h", pq=4),
                in_=xcs[:, 0:2048].rearrange("s (h pq pl) -> s pl pq h", pq=4, pl=16)), [xcsr], [xpr])
            ps, psr = c.psum()

            def f(e):
                for pl in range(16):
                    ins = e.transpose(out=ps[:, pl * NS:(pl + 1) * NS], in_=xperm[:, pl, :], identity=identf[0:NS, 0:NS])
                return ins
            c.op("pe", f, [xpr, identfr], [psr])
            xT, xTr = c.sb([128, 16, NS], F32, "xT", s6t)
            c.op("dve", lambda e: e.tensor_copy(out=xT[:].rearrange("p a s -> p (a s)"), in_=ps[:, 0:16 * NS]), [psr], [xTr])
            xdtT, xdtTr = c.sb([128, 16, NS], F32, "xdtT", s6t)
            c.op("dve", lambda e: e.tensor_tensor(out=xdtT[:], in0=xT[:], in1=dd[:, 0, :].unsqueeze(1).to_broadcast([128, 16, NS]),
                                                  op=ALU.mult), [xTr, ddr], [xdtTr])
            dbg("scs", sc[0:NS, 0:2, :], [NS, 2, 32], [scr])
            dbg("dd", dd[:], [128, 2, NS], [ddr])
            dbg("xcs", xcs[:], [NS, 3072], [xcsr])
            dbg("xT", xT[:], [128, 16, NS], [xTr])
            yT, yTr = c.sb([128, 16, NS], F32, "yT", s6t)
            sel, selr = c.sb([NS, NS * 4 * 128], F32, "sel", s6t)
            c.dma("sp", sel[:], selBG, [], [selr], c.dsem("sel"))
            selv = sel[:].rearrange("k (s g m) -> k s g m", g=4, m=128)
            stb = [c.sb([128, 16, 128], F32, "stb", s6t) for _ in range(2)]
            stsem = [c.dsem("stb") for _ in range(2)]
            stosem = [c.dsem("sto") for _ in range(2)]
            t2b, t2r = c.sb([128, 16, 128], F32, "t2b", s6t)
            bcp, bcpr = c.sb([128, 2, 128], F32, "bcp", s6t)
            for s in range(NS):
                st_, str2 = stb[s % 2]
                str2.multi = False
                for pq in range(4):
                    c.dma("sp", st_[pq * 32:(pq + 1) * 32, :, :], st_ssm[s, :, pq * 16:(pq + 1) * 16, :],
                          [], [str2] if pq == 0 else [], stsem[s % 2])
                str2.w = {stsem[s % 2]: c.cnt[stsem[s % 2]]}
                ps, psr = c.psum()

                def f(e, ps=ps, s=s):
                    for bc in range(2):
                        for g in range(4):
                            col = 2048 + bc * 512 + g * 128
                            ins = e.matmul(ps[:, bc * 128:(bc + 1) * 128], lhsT=selv[:, s, g, :], rhs=xcs[:, col:col + 128],
                                           start=(g == 0), stop=(g == 3))
                    return ins
                c.op("pe", f, [selr, xcsr], [psr])
                c.op("act", lambda e, ps=ps: e.copy(out=bcp[:].rearrange("p a n -> p (a n)"), in_=ps[:, 0:256]), [psr], [bcpr])
                if s == 0:
                    dbg("bcp0", bcp[:], [128, 2, 128], [bcpr])
                c.op("pool", lambda e, s=s: e.tensor_tensor(
                    out=t2b[:], in0=bcp[:, 0, :].unsqueeze(1).to_broadcast([128, 16, 128]),
                    in1=xdtT[:, :, s].unsqueeze(2).to_broadcast([128, 16, 128]), op=ALU.mult), [bcpr, xdtTr], [t2r])
                c.op("dve", lambda e, st_=st_, s=s: e.scalar_tensor_tensor(
                    out=st_[:], in0=st_[:], scalar=dd[:, 1, s:s + 1], in1=t2b[:], op0=ALU.mult, op1=ALU.add),
                    [str2, ddr, t2r], [str2])
                for pq in range(4):
                    c.dma("sp", o_ssms[s, :, pq * 16:(pq + 1) * 16, :], st_[pq * 32:(pq + 1) * 32, :, :], [str2], [], stosem[s % 2])
                c.op("pool", lambda e, st_=st_: e.tensor_tensor(
                    out=t2b[:], in0=st_[:], in1=bcp[:, 1, :].unsqueeze(1).to_broadcast([128, 16, 128]), op=ALU.mult),
                    [str2, bcpr, t2r], [t2r])
                c.op("dve", lambda e, s=s: e.reduce_sum(out=yT[:, :, s], in_=t2b[:], axis=AX.X), [t2r], [yTr])
            for q in stosem:
                ev = Res(); ev.w = {q: c.cnt[q]}; outs_res.append(ev)
            c.op("dve", lambda e: e.scalar_tensor_tensor(out=yT[:], in0=xT[:], scalar=dcol[:, 0:1], in1=yT[:], op0=ALU.mult, op1=ALU.add),
                 [xTr, dcolr, yTr], [yTr])
            ytm, ytmr = c.sb([NS, 2048], F32, "ytm", s6t)
            ytmv = ytm[:].rearrange("s (h pq pl) -> s pl pq h", pq=4, pl=16)
            for q4 in range(4):
                ps, psr = c.psum()

                def f(e, ps=ps, q4=q4):
                    for q in range(4):
                        ins = e.transpose(out=ps[0:NS, q * 128:(q + 1) * 128], in_=yT[:, q4 * 4 + q, :], identity=identf[:])
                    return ins
                c.op("pe", f, [yTr, identfr], [psr])
                c.op("dve", lambda e, ps=ps, q4=q4: e.tensor_copy(
                    out=ytmv[:, q4 * 4:(q4 + 1) * 4], in_=ps[0:NS, :].rearrange("s (pl pq h) -> s pl pq h", pq=4, h=32)),
                    [psr], [ytmr])
            zs_, zsr = c.sb([NS, 2048], F32, "zs", s6t)
            c.dma("sp", zs_[:], PS[:, 0:2048], [PSr], [zsr], c.dsem("zs"))
            ybf, ybfr = c.sb([128, 2048], BF16, "ybfs", s6t)
            gst, gstr = c.sb([128, 4], F32, "gsts", s6t)
            snw, snwr = c.sb([128, 16], F32, "snws", s6t)
            load_vecT(snw[:], ssm_nw.rearrange("(t p) -> t p", p=128), 16, snwr)
            GN_ENV["ybf"], GN_ENV["ybfr"], GN_ENV["gst"], GN_ENV["gstr"], GN_ENV["snw"], GN_ENV["snwr"] = ybf, ybfr, gst, gstr, snw, snwr
            gate_norm_T(NS, ytm[:], ytmr, zs_[:], zsr, lambda hlf: ynTs[:, hlf * 8:(hlf + 1) * 8, :], ynTsr)
        c.barrier()
        s6o.close()

        c.barrier()
        chk(9)

        chk(12)
        n2w, n2r = c.sb([128, 8], F32, "n2w")
        load_vecT(n2w[:], norm2_w.rearrange("(t p) -> t p", p=128), 8, n2r)
        a2, a2r = c.sb([128, 8, 17], F32, "a2")
        c.op("dve", lambda e: e.tensor_scalar(out=a2[:], in0=modT[:, 32:40, :], scalar1=1.0, scalar2=None, op0=ALU.add), [modr], [a2r])
        c.op("dve", lambda e: e.tensor_tensor(out=a2[:], in0=a2[:], in1=n2w[:].unsqueeze(2).to_broadcast([128, 8, 17]), op=ALU.mult),
             [a2r, n2r], [a2r])
        sFF = ExitStack()
        es.enter_context(sFF)
        h2T, _ = c.sb([128, 8, TF], BF16, "h2T", sFF)
        h2res = [Res("h2T%d" % i) for i in range(NFT)]
        h2Ts, h2Tsr = c.sb([128, 8, NS], BF16, "h2Ts", sFF)
        yaTs, yaTsr = yaTs_p, yaTs_pr

        def norm_rows(rows, xt, xr, st, str_, junk, junkr, xn, xnr):
            c.op("act", lambda e: e.activation(out=junk[0:rows, :], in_=xt, func=AF.Square, accum_out=st[0:rows, 0:1]), [xr], [junkr, str_])
            c.op("act", lambda e: e.activation(out=st[0:rows, 1:2], in_=st[0:rows, 0:1], func=AF.Sqrt, scale=1.0 / D, bias=EPS),
                 [str_], [str_])
            c.op("dve", lambda e: e.reciprocal(out=st[0:rows, 2:3], in_=st[0:rows, 1:2]), [str_], [str_])
            c.op("dve", lambda e: e.tensor_scalar(out=xn[0:rows, :], in0=xt, scalar1=st[0:rows, 2:3], scalar2=None, op0=ALU.mult),
                 [xr, str_], [xnr])

        with ExitStack() as s7:
            wso, wsor = c.sb([128, 16, 1024], BF16, "wso", s7)
            wao, waor = c.sb([128, 8, 1024], BF16, "wao", s7)
            wou, wour = c.sb([128, 8, 1024], BF16, "wou", s7)
            for q4 in range(4):
                c.dma("pool", wso[:, q4 * 4:(q4 + 1) * 4, :], wso_d[q4 * 512:(q4 + 1) * 512, :].rearrange("(kt p) c -> p kt c", p=128),
                      [], [wsor], c.dsem("wso"))
            wsor.multi = True
            for q4 in range(2):
                c.dma("pool", wao[:, q4 * 4:(q4 + 1) * 4, :], wao_d[q4 * 512:(q4 + 1) * 512, :].rearrange("(kt p) c -> p kt c", p=128),
                      [], [waor], c.dsem("wao"))
                c.dma("pool", wou[:, q4 * 4:(q4 + 1) * 4, :], wout_d[q4 * 512:(q4 + 1) * 512, :].rearrange("(kt p) c -> p kt c", p=128),
                      [], [wour], c.dsem("wou"))
            waor.multi = True; wour.multi = True
            g1bc, g1bcr = c.sb([128, D], F32, "g1bc", s7)
            c.dma("sp", g1bc[:], MODS[0, 2 * D:3 * D].partition_broadcast(128), [MODSr], [g1bcr], c.dsem("g1bc"))
            g1s, g1sr = c.sb([NS, D], F32, "g1s", s7)
            c.dma("sp", g1s[:], MODS[1:17, 2 * D:3 * D], [MODSr], [g1sr], c.dsem("g1s"))
            ynb, ynbr = c.sb([128, 16, 512], BF16, "ynb", s7)
            yab_, yabr_ = c.sb([128, 8, 512], BF16, "yab7", s7)
            gmb, gmbr = c.sb([128, 16, 512], BF16, "gmb", s7)
            insem = [c.dsem("p7in") for _ in range(3)]
            mgT, mgTr = c.sb([128, 8, 512], BF16, "mgT", s7)
            m1, m1r = c.sb([128, 512], F32, "m1", s7)
            m2, m2r = c.sb([128, 512], F32, "m2", s7)
            xb7 = [c.sb([128, D], F32, "xb7", s7) for _ in range(2)]
            xb7sem = [c.dsem("xb7") for _ in range(2)]
            x1b = [c.sb([128, D], F32, "x1b", s7) for _ in range(2)]
            x1sem = [c.dsem("x1b") for _ in range(2)]
            junk7, junk7r = c.sb([128, D], BF16, "junk7", s7)
            xn7, xn7r = c.sb([128, D], BF16, "xn7", s7)
            st7, st7r = c.sb([128, 4], F32, "st7", s7)
            tf7, tf7r = c.sb([128, 8, 128], F32, "tf7", s7)

            def merged_T(n, yn_fn, ya_fn, gm_fn, reads):
                for et in range(8):
                    ps1, ps1r = c.psum()
                    ps2, ps2r = c.psum()

                    def f(e, ps1=ps1, et=et):
                        for kt in range(16):
                            ins = e.matmul(ps1[:, 0:n], lhsT=wso[:, kt, et * 128:(et + 1) * 128], rhs=yn_fn(kt), start=(kt == 0), stop=(kt == 15))
                        return ins
                    c.op("pe", f, [wsor] + reads, [ps1r])

                    def f(e, ps2=ps2, et=et):
                        for kt in range(8):
                            ins = e.matmul(ps2[:, 0:n], lhsT=wao[:, kt, et * 128:(et + 1) * 128], rhs=ya_fn(kt), start=(kt == 0), stop=(kt == 7))
                        return ins
                    c.op("pe", f, [waor] + reads, [ps2r])
                    c.op("dve", lambda e, ps1=ps1, et=et: e.tensor_tensor(out=m1[:, 0:n], in0=ps1[:, 0:n], in1=gm_fn(et), op=ALU.mult),
                         [ps1r] + reads, [m1r])
                    c.op("dve", lambda e, ps2=ps2, et=et: e.tensor_tensor(out=m2[:, 0:n], in0=ps2[:, 0:n], in1=gm_fn(8 + et), op=ALU.mult),
                         [ps2r] + reads, [m2r])
                    c.op("pool", lambda e, et=et: e.tensor_tensor(out=mgT[:, et, 0:n], in0=m1[:, 0:n], in1=m2[:, 0:n], op=ALU.add),
                         [m1r, m2r], [mgTr])

            def resid_norm2(rows, tt, x_src, gbc, gbcr, x1dst, x1dst_r, h2dst_fn, h2dst_r, a_ap, b_ap, k_):
                xt, xr = xb7[k_ % 2]
                c.dma("sp", xt[0:rows, :], x_src, [], [xr], xb7sem[k_ % 2])
                x1t, x1r_ = x1b[k_ % 2]
                for ec in range(2):
                    ps, psr = c.psum()

                    def f(e, ps=ps, ec=ec):
                        for et in range(8):
                            ins = e.matmul(ps[0:rows, :], lhsT=mgT[:, et, tt * 128:tt * 128 + rows], rhs=wou[:, et, ec * 512:(ec + 1) * 512],
                                           start=(et == 0), stop=(et == 7))
                        return ins
                    c.op("pe", f, [mgTr, wour], [psr])
                    sl = slice(ec * 512, (ec + 1) * 512)
                    c.op("dve", lambda e, ps=ps, sl=sl: e.tensor_tensor(out=x1t[0:rows, sl], in0=ps[0:rows, :], in1=gbc[0:rows, sl], op=ALU.mult),
                         [psr, gbcr], [x1r_])
                c.op("pool", lambda e: e.tensor_tensor(out=x1t[0:rows, :], in0=x1t[0:rows, :], in1=xt[0:rows, :], op=ALU.add), [x1r_, xr], [x1r_])
                c.dma("sp", x1dst, x1t[0:rows, :], [x1r_], [x1dst_r], x1sem[k_ % 2])
                norm_rows(rows, x1t[0:rows, :], x1r_, st7, st7r, junk7, junk7r, xn7, xn7r)
                pt, pr = c.psum()
                ptb = pt[:].bitcast(BF16)

                def f(e, ptb=ptb):
                    for kt in range(8):
                        ins = e.transpose(out=ptb[:, kt * 128:kt * 128 + rows], in_=xn7[0:rows, kt * 128:(kt + 1) * 128],
                                          identity=ident[0:rows, 0:rows])
                    return ins
                c.op("pe", f, [xn7r, identr], [pr])
                pv = ptb.rearrange("p (k c) -> p k c", c=128)[:, :, 0:rows]
                c.op("dve", lambda e: e.tensor_tensor(out=tf7[:, :, 0:rows], in0=pv, in1=a_ap, op=ALU.mult), [pr, a2r], [tf7r])
                c.op("pool", lambda e: e.tensor_tensor(out=h2dst_fn(), in0=tf7[:, :, 0:rows], in1=b_ap, op=ALU.add), [tf7r, modr], [h2dst_r])

            kk_ = 0
            for sc_ in range(5):
                t0 = sc_ * 512
                n = min(512, TF - t0)
                c.dma("sp", ynb[:, :, 0:n], YNT.rearrange("(t p) n -> p t n", p=128)[:, :, t0:t0 + n], [YNTr], [ynbr], insem[0])
                c.dma("sp", yab_[:, :, 0:n], YAT.rearrange("(t p) n -> p t n", p=128)[:, :, t0:t0 + n], [YATr], [yabr_], insem[1])
                c.dma("pool", gmb[:, :, 0:n], GMT.rearrange("(t p) n -> p t n", p=128)[:, :, t0:t0 + n], [GMTr], [gmbr], insem[2])
                merged_T(n, lambda kt: ynb[:, kt, 0:n], lambda kt: yab_[:, kt, 0:n], lambda et: gmb[:, et, 0:n], [ynbr, yabr_, gmbr])
                for tt in range(n // 128):
                    fi = sc_ * 4 + tt
                    vt0 = F0 + fi * 128
                    resid_norm2(128, tt, xv[vt0:vt0 + 128, :], g1bc, g1bcr, X1[fi * 128:(fi + 1) * 128, :], X1r,
                                lambda fi=fi: h2T[:, :, fi * 128:(fi + 1) * 128], h2res[fi],
                                a2[:, :, 0:1].to_broadcast([128, 8, 128]), modT[:, 24:32, 0:1].to_broadcast([128, 8, 128]), kk_)
                    kk_ += 1
            gms, gmsr = c.sb([NS, 2048], F32, "gms", s7)
            c.dma("sp", gms[:], PS[:, 7760:9808], [PSr], [gmsr], c.dsem("gms"))
            c.op("act", lambda e: e.activation(out=gms[:], in_=gms[:], func=AF.Sigmoid), [gmsr], [gmsr])
            gmTs, gmTsr = c.sb([128, 16, NS], F32, "gmTs", s7)
            ps, psr = c.psum()

            def f(e):
                for q in range(16):
                    ins = e.transpose(out=ps[:, q * NS:(q + 1) * NS], in_=gms[:, q * 128:(q + 1) * 128], identity=identf[0:NS, 0:NS])
                return ins
            c.op("pe", f, [gmsr, identfr], [psr])
            c.op("dve", lambda e: e.tensor_copy(out=gmTs[:].rearrange("p a s -> p (a s)"), in_=ps[:, 0:16 * NS]), [psr], [gmTsr])
            merged_T(NS, lambda kt: ynTs[:, kt, :], lambda kt: yaTs[:, kt, :], lambda et: gmTs[:, et, :], [ynTsr, yaTsr, gmTsr])
            resid_norm2(NS, 0, xs, g1s, g1sr, X1s, X1sr, lambda: h2Ts[:], h2Tsr, a2[:, :, 1:17], modT[:, 24:32, 1:17], kk_)
        c.barrier()
        chk(13)

        with ExitStack() as s8:
            wdn, wdnr = c.sb([128, 22, 1024], BF16, "wdn", s8)
            for q in range(11):
                c.dma("pool", wdn[:, 2 * q:2 * q + 2, :], wdn_d[q * 256:(q + 1) * 256, :].rearrange("(kt p) c -> p kt c", p=128),
                      [], [wdnr], c.dsem("wdn"))
            wdnr.multi = True
            fcw, fcwr = c.sb([128, 3, 44], F32, "fcw", s8)
            fcb, fcbr = c.sb([128, 44], F32, "fcb", s8)
            for kq in range(3):
                fr = Res("fcw%d" % kq)
                load_vecT(fcw[:, kq, :], fcw_d[kq].rearrange("(t p) -> t p", p=128), 44, fr)
                fcwr.w.update(fr.w)
            load_vecT(fcb[:], fcb_d.rearrange("(t p) -> t p", p=128), 44, fcbr)
            g2bc, g2bcr = c.sb([128, D], F32, "g2bc", s8)
            c.dma("sp", g2bc[:], MODS[0, 5 * D:6 * D].partition_broadcast(128), [MODSr], [g2bcr], c.dsem("g2bc"))
            g2s, g2sr = c.sb([NS, D], F32, "g2s", s8)
            c.dma("sp", g2s[:], MODS[1:17, 5 * D:6 * D], [MODSr], [g2sr], c.dsem("g2s"))
            fwbc, fwbcr = c.sb([128, D], F32, "fwbc", s8)
            c.dma("sp", fwbc[:], final_w.partition_broadcast(128), [], [fwbcr], c.dsem("fwbc"))
            fosem = c.dsem("ffno")
            x8 = [c.sb([128, D], F32, "x8", s8) for _ in range(2)]
            x8sem = [c.dsem("x8") for _ in range(2)]
            y8 = [c.sb([128, D], F32, "y8", s8) for _ in range(2)]
            y8sem = [c.dsem("y8") for _ in range(2)]
            junk8, junk8r = c.sb([128, D], BF16, "junk8", s8)
            st8, st8r = c.sb([128, 4], F32, "st8", s8)
            HT = TF // 2
            s8a = ExitStack()
            es.enter_context(s8a)
            aT, aTr = c.sb([128, 22, HT], BF16, "aT", s8a)
            uhist, uhistr = c.sb([128, 44, 2], F32, "uhist", s8a)
            c.op("pool", lambda e: e.memset(uhist[:], 0.0), [], [uhistr])
            stg8 = [c.sb([NS, 128], F32, "stg8", s8a) for _ in range(4)]
            stg8sem = [c.dsem("stg8") for _ in range(4)]
            sq8 = rr(4)
            wu = [c.sb([128, 8, 256], BF16, "wu", s8a) for _ in range(4)]
            wusem = [c.dsem("wu") for _ in range(4)]
            wuq = rr(4)
            ur8 = [c.sb([128, 2 + HT], F32, "ur8", s8a) for _ in range(2)]
            ac8 = [c.sb([128, HT], F32, "ac8", s8a) for _ in range(2)]

            def final_rows(rows, x1src, x1src_r, gbc, gbcr, aT_fn, reads, dst, k_):
                xt, xr = x8[k_ % 2]
                c.dma("sp", xt[0:rows, :], x1src, [x1src_r], [xr], x8sem[k_ % 2])
                yt, yr = y8[k_ % 2]
                for ec in range(2):
                    ps, psr = c.psum()

                    def f(e, ps=ps, ec=ec):
                        for ct in range(22):
                            ins = e.matmul(ps[0:rows, :], lhsT=aT_fn(ct), rhs=wdn[:, ct, ec * 512:(ec + 1) * 512], start=(ct == 0), stop=(ct == 21))
                        return ins
                    c.op("pe", f, [wdnr] + reads, [psr])
                    sl = slice(ec * 512, (ec + 1) * 512)
                    c.op("dve", lambda e, ps=ps, sl=sl: e.tensor_tensor(out=yt[0:rows, sl], in0=ps[0:rows, :], in1=gbc[0:rows, sl], op=ALU.mult),
                         [psr, gbcr], [yr])
                c.op("pool", lambda e: e.tensor_tensor(out=yt[0:rows, :], in0=yt[0:rows, :], in1=xt[0:rows, :], op=ALU.add), [yr, xr], [yr])
                c.op("act", lambda e: e.activation(out=junk8[0:rows, :], in_=yt[0:rows, :], func=AF.Square, accum_out=st8[0:rows, 0:1]),
                     [yr], [junk8r, st8r])
                c.op("act", lambda e: e.activation(out=st8[0:rows, 1:2], in_=st8[0:rows, 0:1], func=AF.Sqrt, scale=1.0 / D, bias=EPS),
                     [st8r], [st8r])
                c.op("dve", lambda e: e.reciprocal(out=st8[0:rows, 2:3], in_=st8[0:rows, 1:2]), [st8r], [st8r])
                c.op("dve", lambda e: e.scalar_tensor_tensor(out=yt[0:rows, :], in0=yt[0:rows, :], scalar=st8[0:rows, 2:3], in1=fwbc[0:rows, :],
                                                             op0=ALU.mult, op1=ALU.mult), [yr, st8r, fwbcr], [yr])
                if dst is not None:
                    c.dma("sp", dst, yt[0:rows, :], [yr], [], y8sem[k_ % 2])

            kk_ = 0
            wu_cur = {}
            for hf in range(2):
                tb = hf * HT
                for ct in range(22):
                    rows_u = []
                    for part in range(2):
                        col0 = part * 2816 + ct * 128
                        if ct % 2 == 0:
                            j = next(wuq)
                            wfull, wr = wu[j]
                            c.dma("pool", wfull[:], wup_d[:, col0:col0 + 256].rearrange("(kt p) c -> p kt c", p=128), [], [wr], wusem[j])
                            wu_cur[part] = (wfull, wr)
                        wfull, wr = wu_cur[part]
                        wt = wfull[:, :, (ct % 2) * 128:(ct % 2) * 128 + 128]
                        u, ur = ur8[part]
                        ci = part * 22 + ct
                        c.op("pool", lambda e, u=u, ci=ci: e.tensor_copy(out=u[:, 0:2], in_=uhist[:, ci, :]), [uhistr, ur], [ur])
                        for o0 in range(0, HT, 512):
                            n = min(512, HT - o0)
                            ps, psr = c.psum()
                            mm8(ps[:, 0:n], lambda kt, wt=wt: wt[:, kt, :], lambda kt, o0=o0, n=n: h2T[:, kt, tb + o0:tb + o0 + n],
                                [wr] + h2res[(tb + o0) // 128:(tb + o0 + n) // 128], [psr])
                            evac(u[:, 2 + o0:2 + o0 + n], ps[:, 0:n], [psr], [ur])
                        if hf == 0:
                            ps, psr = c.psum()
                            mm8(ps[0:NS, 0:128], lambda kt: h2Ts[:, kt, :], lambda kt, wt=wt: wt[:, kt, :], [h2Tsr, wr], [psr])
                            j8 = next(sq8)
                            sg8, sg8r = stg8[j8]
                            evac(sg8[:], ps[0:NS, 0:128], [psr], [sg8r])
                            c.dma("sp", US[:, col0:col0 + 128], sg8[:], [sg8r], [USr], stg8sem[j8])
                            c.op("pool", lambda e, u=u: e.tensor_scalar(out=u[:, 2:2 + 256], in0=u[:, 2:2 + 256], scalar1=pf[:, 0:1],
                                                                       scalar2=None, op0=ALU.mult), [ur, pfr], [ur])
                        else:
                            c.dma("sp", o_ffnp[:, col0:col0 + 128].rearrange("k c -> c k"), u[:, HT:HT + 2], [ur], [], fosem,
                                  allow_slow_non_contiguous=True)
                        c.op("pool", lambda e, u=u, ci=ci: e.tensor_copy(out=uhist[:, ci, :], in_=u[:, HT:HT + 2]), [ur, uhistr], [uhistr])
                        a, ar = ac8[part]
                        c.op("dve", lambda e, a=a, u=u, ci=ci: e.tensor_scalar(out=a[:], in0=u[:, 0:HT], scalar1=fcw[:, 0, ci:ci + 1],
                                                                              scalar2=fcb[:, ci:ci + 1], op0=ALU.mult, op1=ALU.add),
                             [ur, fcwr, fcbr], [ar])
                        c.op("dve", lambda e, a=a, u=u, ci=ci: e.scalar_tensor_tensor(out=a[:], in0=u[:, 1:HT + 1], scalar=fcw[:, 1, ci:ci + 1],
                                                                                     in1=a[:], op0=ALU.mult, op1=ALU.add), [ur, fcwr, ar], [ar])
                        c.op("dve", lambda e, a=a, u=u, ci=ci: e.scalar_tensor_tensor(out=a[:], in0=u[:, 2:HT + 2], scalar=fcw[:, 2, ci:ci + 1],
                                                                                     in1=a[:], op0=ALU.mult, op1=ALU.add), [ur, fcwr, ar], [ar])
                        rows_u.append((a, ar))
                    (a1_, a1r_), (a2_, a2r_) = rows_u
                    c.op("act", lambda e, a1_=a1_: e.activation(out=a1_[:], in_=a1_[:], func=AF.Silu), [a1r_], [a1r_])
                    c.op("pool", lambda e, a1_=a1_, a2_=a2_, ct=ct: e.tensor_tensor(out=aT[:, ct, :], in0=a1_[:], in1=a2_[:], op=ALU.mult),
                         [a1r_, a2r_], [aTr])
                for tt in range(HT // 128):
                    fi = hf * (HT // 128) + tt
                    vt0 = F0 + fi * 128
                    dst = o_yp[vt0 - 2048:vt0 - 2048 + 128, :] if vt0 >= 2048 else None
                    final_rows(128, X1[fi * 128:(fi + 1) * 128, :], X1r, g2bc, g2bcr,
                               lambda ct, tt=tt: aT[:, ct, tt * 128:(tt + 1) * 128], [aTr], dst, kk_)
                    kk_ += 1
            for q in y8sem + [fosem]:
                ev = Res(); ev.w = {q: c.cnt[q]}; outs_res.append(ev)
            c.barrier()
            s8a.close()
            o_sem = c.dsem("ffns")
            usl, uslr = c.sb([NS, 1408], F32, "usl", s8)
            uslsem = c.dsem("usl")
            hst, hstr = c.sb([NS, 2, 1408], F32, "hst", s8)
            wcs_, wcsr = c.sb([NS, 3, 1408], F32, "wcs", s8)
            bcs_, bcsr = c.sb([NS, 1408], F32, "bcs", s8)
            as_, asr = c.sb([NS, 5632], F32, "as", s8)
            asr.multi = True
            hsem = [c.dsem("hst") for _ in range(3)]
            ho = c.dsem("hsto")
            for cc in range(4):
                sl = slice(cc * 1408, (cc + 1) * 1408)
                c.dma("sp", hst[:], st_ffn[:, :, sl], [], [hstr], hsem[0])
                c.dma("sp", wcs_[:], fcw_d[:, sl].partition_broadcast(NS), [], [wcsr], hsem[1])
                c.dma("sp", bcs_[:], fcb_d[sl].partition_broadcast(NS), [], [bcsr], hsem[2])
                c.dma("sp", o_ffns[:, 0, sl], hst[:, 1, :], [hstr], [], ho)
                c.dma("sp", usl[:], US[:, sl], [USr], [uslr], uslsem)
                c.dma("sp", o_ffns[:, 1, sl], usl[:], [uslr], [], o_sem)
                c.op("dve", lambda e: e.tensor_tensor(out=hst[:], in0=hst[:], in1=wcs_[:, 0:2, :], op=ALU.mult), [hstr, wcsr], [hstr])
                c.op("dve", lambda e: e.tensor_tensor(out=hst[:, 0, :], in0=hst[:, 0, :], in1=hst[:, 1, :], op=ALU.add), [hstr], [hstr])
                c.op("dve", lambda e: e.tensor_tensor(out=hst[:, 1, :], in0=usl[:], in1=wcs_[:, 2, :], op=ALU.mult),
                     [uslr, wcsr, hstr], [hstr])
                c.op("dve", lambda e: e.tensor_tensor(out=hst[:, 0, :], in0=hst[:, 0, :], in1=hst[:, 1, :], op=ALU.add), [hstr], [hstr])
                c.op("dve", lambda e, sl=sl: e.tensor_tensor(out=as_[:, sl], in0=hst[:, 0, :], in1=bcs_[:], op=ALU.add),
                     [hstr, bcsr], [asr])
            for q in (o_sem, ho):
                ev = Res(); ev.w = {q: c.cnt[q]}; outs_res.append(ev)
            asr2 = Res("asr2")
            c.op("act", lambda e: e.activation(out=as_[:, 0:2816], in_=as_[:, 0:2816], func=AF.Silu), [asr], [asr2])
            abf, abfr = c.sb([NS, 2816], BF16, "abf", s8)
            c.op("dve", lambda e: e.tensor_tensor(out=abf[:], in0=as_[:, 0:2816], in1=as_[:, 2816:5632], op=ALU.mult), [asr, asr2], [abfr])
            aTs, aTsr = c.sb([128, 22, NS], BF16, "aTs", s8)
            for q2 in range(2):
                pt, pr = c.psum()
                ptb = pt[:].bitcast(BF16)

                def f(e, ptb=ptb, q2=q2):
                    for q in range(11):
                        ct = q2 * 11 + q
                        ins = e.transpose(out=ptb[:, q * NS:(q + 1) * NS], in_=abf[:, ct * 128:(ct + 1) * 128], identity=ident[0:NS, 0:NS])
                    return ins
                c.op("pe", f, [abfr, identr], [pr])
                c.op("dve", lambda e, ptb=ptb, q2=q2: e.tensor_copy(out=aTs[:, q2 * 11:(q2 + 1) * 11, :].rearrange("p a s -> p (a s)"),
                                                                   in_=ptb[:, 0:11 * NS]), [pr], [aTsr])
            final_rows(NS, X1s, X1sr, g2s, g2sr, lambda ct: aTs[:, ct, :], [aTsr], o_ys, kk_)
            ev = Res(); ev.w = {y8sem[kk_ % 2]: c.cnt[y8sem[kk_ % 2]]}; outs_res.append(ev)
        c.barrier()
        sFF.close()
        chk(14)
        c.wait_all("sp", outs_res)
    except _Stop:
        pass
    return nc


def _selbg():
    s = np.zeros((NS, NS, 4, 128), np.float32)
    for sp in range(NS):
        for g in range(4):
            for m in range(128):
                if (m % 32) // 8 == g:
                    s[sp, sp, g, m] = 1.0
    return s.reshape(NS, NS * 4 * 128)


def _att_consts(half):
    ii = np.arange(NFT)[:, None, None]
    tl = np.arange(128)[None, None, :]
    n = np.arange(256)[None, :, None]
    t = (F0T + ii) * 128 + tl
    maskC = (16 * n + 31 <= t).astype(np.float32)
    if half == 0:
        maskC = maskC * (n >= 128)
    maskC = maskC.reshape(NFT, 2, 128, 128)
    tq = ((F0T + np.arange(NFT))[:, None, None] * 128 + np.arange(128)[None, :, None])
    j = np.arange(64)[None, None, :]
    valid = (64 * j <= tq) & ((j >= 32) | (half == 1))
    first = 0 if half == 1 else 32
    cur = tq // 64
    forced = (j == first) | (j == cur) | (j == cur - 1)
    keep = (valid & ~forced).astype(np.float32)
    impA = np.where(valid, np.where(forced, 1e4, 0.0), -1e4).astype(np.float32)
    selc = np.stack([keep, impA, valid.astype(np.float32)], axis=2)
    nn = np.arange(256)[:, None] * 16
    ss = np.arange(64)[None, :] * 64
    ov = np.clip(np.minimum(nn + 32, ss + 64) - np.maximum(nn, ss), 0, None) / 16.0
    ov[255] = 0.0
    return maskC, np.ascontiguousarray(selc), ov.astype(np.float32)


def _sample_consts():
    j = np.arange(64)
    valid = (j <= 32)
    forced = (j == 0) | (j == 31) | (j == 32)
    keep = (valid & ~forced).astype(np.float32)
    impA = np.where(valid, np.where(forced, 1e4, 0.0), -1e4).astype(np.float32)
    sconst = np.concatenate([keep, impA, valid.astype(np.float32)])[None, :]
    p = np.arange(128)
    smask = np.stack([(p == 0), (p != 0), (p != 127)], axis=1).astype(np.float32)
    nn = np.arange(128)[:, None] * 16
    ss = np.arange(64)[None, :] * 64
    ov = np.clip(np.minimum(nn + 32, ss + 64) - np.maximum(nn, ss), 0, None) / 16.0
    ov[127] = 0.0
    ovs = np.concatenate([ov, np.ones((128, 1))], axis=1).astype(np.float32)
    return sconst.astype(np.float32), smask, ovs


OUT_NAMES = ["o_kvp", "o_kvs", "o_winp", "o_wins", "o_convp", "o_convs"]


def kernel(x_prompt, x_sample, c_prompt, c_sample, cache_nsa_kv, page_table, cache_win_kv, state_ssm,
           state_ssm_conv, state_ffn_conv, ada_w, ada_b, norm1_w, norm2_w, final_norm_w, w_in,
           ssm_conv_w, ssm_conv_b, ssm_dt_bias, ssm_A_log, ssm_D, ssm_norm_w,
           cmp_pe_k, cmp_w1_k, cmp_w2_k, cmp_pe_v, cmp_w1_v, cmp_w2_v,
           w_ssm_out, w_att_out, w_out, ffn_w_up, ffn_conv_w, ffn_conv_b, ffn_w_down):
    f32 = np.float32
    A = lambda a: np.ascontiguousarray(np.asarray(a))
    x_prompt = A(x_prompt); x_sample = A(x_sample)
    nc = build_program()
    half_f = 8
    inv_freq = (500000.0 ** (-np.arange(half_f, dtype=np.float32) / half_f)).astype(f32)
    in_maps = []
    cache2d = A(cache_nsa_kv).reshape(2560 * 128, 1024)
    for core in range(8):
        b, half = core // 2, core % 2
        xvv = np.concatenate([x_prompt[b, 0:2048], x_prompt[b, half * 2048:(half + 1) * 2048]], axis=0)
        pos = (np.arange(TV) - (0 if half == 1 else 2048)).astype(f32)
        ang = pos[:, None] * inv_freq[None, :]
        angs = np.full((NS, 1), 2048.0, f32) * inv_freq[None, :]
        sl = slice(core * NS, (core + 1) * NS)
        m = {
            "xv": A(xvv), "xs": A(x_sample[sl, 0]),
            "cin": A(np.concatenate([np.asarray(c_prompt)[b:b + 1], np.asarray(c_sample)[sl]], axis=0)),
            "ada_w": A(ada_w), "ada_b": A(ada_b), "norm1_w": A(norm1_w), "norm2_w": A(norm2_w),
            "final_norm_w": A(final_norm_w), "w_in": A(w_in),
            "ssm_conv_w": A(ssm_conv_w), "ssm_conv_b": A(ssm_conv_b),
            "st_conv": A(np.asarray(state_ssm_conv)[sl]),
            "csv": np.concatenate([np.cos(ang), np.sin(ang)], axis=1).astype(f32),
            "css": np.concatenate([np.cos(angs), np.sin(angs)], axis=1).astype(f32),
            "pf": np.full((128, 1), float(half), f32),
            "cwin": A(np.asarray(cache_win_kv)[sl].reshape(NS, 512, 512)),
            "ssm_dt_bias": A(ssm_dt_bias), "ssm_A_log": A(ssm_A_log), "ssm_D": A(ssm_D), "ssm_norm_w": A(ssm_norm_w),
            "st_ssm": A(np.asarray(state_ssm)[sl]), "selBG": _selbg(),
        }
        mC, sC, ovm = _att_consts(half)
        m.update({"cmp_w1_k": A(cmp_w1_k), "cmp_w1_v": A(cmp_w1_v), "cmp_w2_k": A(cmp_w2_k), "cmp_w2_v": A(cmp_w2_v),
                  "cmp_pe_k": A(cmp_pe_k), "cmp_pe_v": A(cmp_pe_v), "ovm": ovm, "maskC": mC, "selc": sC})
        m.update({"w_ssm_out": A(w_ssm_out), "w_att_out": A(w_att_out), "w_out": A(w_out), "ffn_w_up": A(ffn_w_up),
                  "ffn_w_down": A(ffn_w_down), "ffn_conv_w": A(ffn_conv_w), "ffn_conv_b": A(ffn_conv_b),
                  "st_ffn": A(np.asarray(state_ffn_conv)[sl])})
        scst, smk_, ovs_ = _sample_consts()
        m.update({"cache_nsa_kv": cache2d, "ptab": A(np.asarray(page_table)[sl].reshape(-1).astype(np.int32)),
                  "iotap": np.arange(128, dtype=f32)[:, None], "sconst": scst, "smask": smk_, "ovs": ovs_})
        in_maps.append(m)
    res = run_bass_kernel_spmd(nc, in_maps, core_ids=list(range(8)))
    R = res.results
    B, T = 4, 4096
    y_prompt = np.zeros((B, T, D), f32); y_sample = np.zeros((128, 1, D), f32)
    kv_p = np.zeros((B, T, 4, 4, 64), f32); kv_s = np.zeros((128, 1, 4, 4, 64), f32)
    win_p = np.zeros((B, 512, 2, 4, 64), f32); win_s = np.zeros((128, 512, 2, 4, 64), f32)
    ssm_p = np.zeros((B, 32, 64, 128), f32); ssm_s = np.zeros((128, 32, 64, 128), f32)
    conv_p = np.zeros((B, 3, 3072), f32); conv_s = np.zeros((128, 3, 3072), f32)
    ffn_p = np.zeros((B, 2, 5632), f32); ffn_s = np.zeros((128, 2, 5632), f32)
    for core in range(8):
        b, half = core // 2, core % 2
        r = R[core]
        sl = slice(core * NS, (core + 1) * NS)
        kv_p[b, half * 2048:(half + 1) * 2048] = r["o_kvp"].reshape(2048, 4, 4, 64)
        kv_s[sl, 0] = r["o_kvs"].reshape(NS, 4, 4, 64)
        win_s[sl] = r["o_wins"].reshape(NS, 512, 2, 4, 64)
        conv_s[sl] = r["o_convs"]
        ssm_s[sl] = r["o_ssms"]
        y_sample[sl, 0] = r["o_ys"]
        ffn_s[sl] = r["o_ffns"]
        y_prompt[b, half * 2048:(half + 1) * 2048] = r["o_yp"]
        if half == 1:
            win_p[b] = r["o_winp"].reshape(512, 2, 4, 64)
            conv_p[b] = r["o_convp"]
            ssm_p[b] = r["o_ssmp"].reshape(32, 64, 128)
            ffn_p[b] = r["o_ffnp"]
    return (y_prompt, y_sample, kv_p, kv_s, win_p, win_s, ssm_p, ssm_s, conv_p, conv_s, ffn_p, ffn_s)
```
